# Optimizing a Trainium2 kernel written in Bass

```python
import jax, jax.numpy as jnp
from jax import lax
import numpy as np

D_MODEL = 1024
BATCH = 1
SEQ = 16384
DEPTH = 1
DEC_BATCH = 8
DEC_SEQ = 4096
PAST_LEN = 128

N_FOURIER_GROUPS = 4
FOURIER_GROUP_DIM = 128
FOURIER_WIDTH = N_FOURIER_GROUPS * FOURIER_GROUP_DIM
N_HEADS = 8
QK_NOPE_DIM = 64
QK_ROPE_DIM = 32
QK_HEAD_DIM = QK_NOPE_DIM + QK_ROPE_DIM
V_HEAD_DIM = 64
ATTN_WIDTH = N_HEADS * V_HEAD_DIM
Q_LORA_RANK = 384
KV_LORA_RANK = 256
ROPE_THETA = 10000.0
Q_BLOCK = 128
N_BRANCHES = 2
D_FF = -(-8 * D_MODEL // (3 * 256)) * 256
LN_EPS = 1e-5
RMS_EPS = 1e-6
DEEPNORM_ALPHA = (2.0 * DEPTH) ** 0.25
DEEPNORM_BETA = (8.0 * DEPTH) ** -0.25
IN_PROJ_DIM = FOURIER_WIDTH + Q_LORA_RANK + KV_LORA_RANK + QK_ROPE_DIM + N_BRANCHES * D_MODEL

kernel_name = "fnet_mla_gated_deepnorm_encoder"


def layer_norm(x, g, b):
    xf = x.astype(jnp.float32)
    mu = jnp.mean(xf, axis=-1, keepdims=True)
    var = jnp.mean(jnp.square(xf - mu), axis=-1, keepdims=True)
    y = (xf - mu) * lax.rsqrt(var + LN_EPS) * g.astype(jnp.float32) + b.astype(jnp.float32)
    return y.astype(x.dtype)


def rms_norm(x, g):
    xf = x.astype(jnp.float32)
    y = xf * lax.rsqrt(jnp.mean(jnp.square(xf), axis=-1, keepdims=True) + RMS_EPS) * g.astype(jnp.float32)
    return y.astype(x.dtype)


def rope(x, seq_len):
    half = QK_ROPE_DIM // 2
    inv_freq = 1.0 / (ROPE_THETA ** (jnp.arange(0, QK_ROPE_DIM, 2, dtype=jnp.float32) / QK_ROPE_DIM))
    ang = jnp.arange(seq_len, dtype=jnp.float32)[:, None] * inv_freq[None, :]
    cos = jnp.cos(ang)[None, :, None, :].astype(x.dtype)
    sin = jnp.sin(ang)[None, :, None, :].astype(x.dtype)
    x1, x2 = x[..., :half], x[..., half:]
    return jnp.concatenate([x1 * cos - x2 * sin, x2 * cos + x1 * sin], axis=-1)


def fourier_mix(u):
    b, s, _ = u.shape
    ug = u.astype(jnp.float32).reshape(b, s, N_FOURIER_GROUPS, FOURIER_GROUP_DIM)
    f = jnp.fft.fft2(ug, axes=(1, 3), norm="ortho").real
    return f.reshape(b, s, FOURIER_WIDTH).astype(u.dtype)


def mla_attention(c_q, c_kv, k_rope, g_q, w_uq, g_kv, w_ukv):
    b, s, _ = c_q.shape
    q = (rms_norm(c_q, g_q) @ w_uq).reshape(b, s, N_HEADS, QK_HEAD_DIM)
    q = jnp.concatenate([q[..., :QK_NOPE_DIM], rope(q[..., QK_NOPE_DIM:], s)], axis=-1)
    kv = (rms_norm(c_kv, g_kv) @ w_ukv).reshape(b, s, N_HEADS, QK_NOPE_DIM + V_HEAD_DIM)
    k_nope, v = kv[..., :QK_NOPE_DIM], kv[..., QK_NOPE_DIM:]
    k_pe = rope(k_rope[:, :, None, :], s)
    k = jnp.concatenate([k_nope, jnp.broadcast_to(k_pe, (b, s, N_HEADS, QK_ROPE_DIM))], axis=-1)
    scale = QK_HEAD_DIM ** -0.5
    n_blk = s // Q_BLOCK
    q_blocks = q.reshape(b, n_blk, Q_BLOCK, N_HEADS, QK_HEAD_DIM).transpose(1, 0, 2, 3, 4)

    def attend(qb):
        sc = jnp.einsum('bqhd,bkhd->bhqk', qb, k, preferred_element_type=jnp.float32) * scale
        p = jax.nn.softmax(sc, axis=-1).astype(v.dtype)
        return jnp.einsum('bhqk,bkhd->bqhd', p, v)

    o = lax.map(attend, q_blocks)
    return o.transpose(1, 0, 2, 3, 4).reshape(b, s, ATTN_WIDTH)


def encoder_layer(x, w_in, w_fourier, g_q, w_uq, g_kv, w_ukv, w_attn, w_o,
                  ln1_g, ln1_b, w_gate, w_up, w_down, ln2_g, ln2_b):
    b, s, _ = x.shape
    proj = x @ w_in
    o0 = FOURIER_WIDTH
    o1 = o0 + Q_LORA_RANK
    o2 = o1 + KV_LORA_RANK
    o3 = o2 + QK_ROPE_DIM
    u_f, c_q, c_kv, k_rope, gates = proj[..., :o0], proj[..., o0:o1], proj[..., o1:o2], proj[..., o2:o3], proj[..., o3:]
    branch_a = fourier_mix(u_f) @ w_fourier
    branch_b = mla_attention(c_q, c_kv, k_rope, g_q, w_uq, g_kv, w_ukv) @ w_attn
    g = jax.nn.sigmoid(gates.astype(jnp.float32)).astype(x.dtype).reshape(b, s, N_BRANCHES, D_MODEL)
    merged = g[:, :, 0] * branch_a + g[:, :, 1] * branch_b
    h = layer_norm(DEEPNORM_ALPHA * x + merged @ w_o, ln1_g, ln1_b)
    ffn = (jax.nn.silu(h @ w_gate) * (h @ w_up)) @ w_down
    return layer_norm(DEEPNORM_ALPHA * h + ffn, ln2_g, ln2_b)


def setup_inputs(seed: int = 0) -> dict:
    key = jax.random.key(seed)
    ks = jax.random.split(key, 18)
    f32 = jnp.float32

    def w(k, shape, fan_in, mult=1.0):
        return jax.random.normal(k, (DEPTH,) + shape, f32) * (fan_in ** -0.5) * mult

    def gain(k, n):
        return 1.0 + 0.02 * jax.random.normal(k, (DEPTH, n), f32)

    def bias(k, n):
        return 0.02 * jax.random.normal(k, (DEPTH, n), f32)

    return {
        "x_prompt": jax.random.normal(ks[0], (BATCH, SEQ, D_MODEL), f32),
        "x_sample": jax.random.normal(ks[1], (DEC_BATCH, DEC_SEQ, D_MODEL), f32),
        "w_in": w(ks[2], (D_MODEL, IN_PROJ_DIM), D_MODEL),
        "w_fourier": w(ks[3], (FOURIER_WIDTH, D_MODEL), FOURIER_WIDTH),
        "g_q": gain(ks[4], Q_LORA_RANK),
        "w_uq": w(ks[5], (Q_LORA_RANK, N_HEADS * QK_HEAD_DIM), Q_LORA_RANK),
        "g_kv": gain(ks[6], KV_LORA_RANK),
        "w_ukv": w(ks[7], (KV_LORA_RANK, N_HEADS * (QK_NOPE_DIM + V_HEAD_DIM)), KV_LORA_RANK),
        "w_attn": w(ks[8], (ATTN_WIDTH, D_MODEL), ATTN_WIDTH),
        "w_o": w(ks[9], (D_MODEL, D_MODEL), D_MODEL, DEEPNORM_BETA),
        "ln1_g": gain(ks[10], D_MODEL),
        "ln1_b": bias(ks[11], D_MODEL),
        "w_gate": w(ks[12], (D_MODEL, D_FF), D_MODEL),
        "w_up": w(ks[13], (D_MODEL, D_FF), D_MODEL),
        "w_down": w(ks[14], (D_FF, D_MODEL), D_FF, DEEPNORM_BETA),
        "ln2_g": gain(ks[15], D_MODEL),
        "ln2_b": bias(ks[16], D_MODEL),
    }


def _trunk(x, w_in, w_fourier, g_q, w_uq, g_kv, w_ukv, w_attn, w_o,
           ln1_g, ln1_b, w_gate, w_up, w_down, ln2_g, ln2_b):
    for l in range(DEPTH):
        x = encoder_layer(x, w_in[l], w_fourier[l], g_q[l], w_uq[l], g_kv[l], w_ukv[l], w_attn[l], w_o[l],
                          ln1_g[l], ln1_b[l], w_gate[l], w_up[l], w_down[l], ln2_g[l], ln2_b[l])
    return x


def reference(x_prompt, x_sample, w_in, w_fourier, g_q, w_uq, g_kv, w_ukv, w_attn, w_o,
              ln1_g, ln1_b, w_gate, w_up, w_down, ln2_g, ln2_b):
    y_prompt = _trunk(x_prompt, w_in, w_fourier, g_q, w_uq, g_kv, w_ukv, w_attn, w_o,
                      ln1_g, ln1_b, w_gate, w_up, w_down, ln2_g, ln2_b)
    y_sample = _trunk(x_sample, w_in, w_fourier, g_q, w_uq, g_kv, w_ukv, w_attn, w_o,
                      ln1_g, ln1_b, w_gate, w_up, w_down, ln2_g, ln2_b)
    return (y_prompt, y_sample)
```

```python
import os
import numpy as np
import ml_dtypes
from contextlib import ExitStack
import concourse.bass as bass
import concourse.mybir as mybir
from concourse.bass_utils import run_bass_kernel_spmd

F32 = mybir.dt.float32
BF16 = mybir.dt.bfloat16
ALU = mybir.AluOpType
AF = mybir.ActivationFunctionType

D = 1024
DFF = 2816
NF = 22
NH = 8
SCALE = 96 ** -0.5
ALPHA = 2.0 ** 0.25
LN_EPS = 1e-5
RMS_EPS = 1e-6
NCORES = 8


class Reg:
    __slots__ = ("name", "w", "r")

    def __init__(self, name):
        self.name = name
        self.w = None
        self.r = {}


class MultiReg:
    def __init__(self, name):
        self.name = name
        self.members = []

    def new(self):
        r = Reg(self.name)
        self.members.append(r)
        return r


def _flat(regs):
    out = []
    for r in regs:
        if isinstance(r, MultiReg):
            out.extend(r.members)
        else:
            out.append(r)
    return out


class Prog:
    ENGS = ["pe", "act", "dve", "pool", "sp"]

    def __init__(self, nc):
        self.nc = nc
        self.ops = {e: [] for e in self.ENGS}
        self.cnt = {}
        self.waited = {e: {} for e in self.ENGS}
        self.nwaits = 0
        self.epoch = 0

    def op(self, eng, fn, reads=(), writes=(), dma=None):
        reads = _flat(reads)
        deps = {}

        def add(dep):
            if dep is None:
                return
            k, v = dep
            if deps.get(k, 0) < v:
                deps[k] = v

        for r in reads:
            add(r.w)
        for w in writes:
            add(w.w)
            for d in w.r.items():
                add(d)
        key = dma if dma else "E_%s_%d" % (eng, self.epoch)
        inc = 16 if dma else 1
        val = self.cnt.get(key, 0) + inc
        self.cnt[key] = val
        waits = []
        wd = self.waited[eng]
        for k, v in deps.items():
            if eng == "pe" and k.startswith("E_pe_"):
                continue
            if wd.get(k, 0) < v:
                wd[k] = v
                waits.append((k, v))
        self.nwaits += len(waits)
        self.ops[eng].append((waits, fn, key, inc))
        for r in reads:
            if r.r.get(key, 0) < val:
                r.r[key] = val
        for w in writes:
            w.w = (key, val)
            w.r = {}
        return (key, val)

    def uniq(self, base):
        self._u = getattr(self, "_u", 0) + 1
        return "%s%d" % (base, self._u)

    def barrier(self):
        for e in self.ENGS:
            waits = []
            wd = self.waited[e]
            for k, v in self.cnt.items():
                if k.startswith("E_" + e + "_"):
                    continue
                if wd.get(k, 0) < v:
                    wd[k] = v
                    waits.append((k, v))
            if waits:
                self.ops[e].append((waits, None, None, 0))
        self.epoch += 1

    def emit(self):
        nc = self.nc
        self.barrier()
        with ExitStack() as es:
            sems = {k: es.enter_context(nc.semaphore(k)) for k in self.cnt}
            block = es.enter_context(nc.Block())

            def body(engname):
                def f(eng):
                    for waits, fn, key, inc in self.ops[engname]:
                        for (k, v) in waits:
                            eng.wait_ge(sems[k], v)
                        if fn is not None:
                            ins = getattr(eng, fn[0])(**fn[1])
                            ins.then_inc(sems[key], inc)
                return f

            block.tensor(body("pe"))
            block.scalar(body("act"))
            block.vector(body("dve"))
            block.gpsimd(body("pool"))
            block.sync(body("sp"))
        print("ops:", {e: len(v) for e, v in self.ops.items()}, "waits:", self.nwaits,
              "maxcnt:", max(self.cnt.values()), "nsems:", len(self.cnt), flush=True)


class Arena:
    BASE = 17408

    def __init__(self, nc, nbytes):
        self.nc = nc
        self.free = [(0, nbytes)]
        self.peak = 0
        self.used = 0
        self.n = 0

    def alloc(self, shape, dt):
        n = 1
        for d in shape[1:]:
            n *= d
        nb = n * (4 if dt == F32 else 2)
        nb = (nb + 255) // 256 * 256
        for k, (o, sz) in enumerate(self.free):
            if sz >= nb:
                if sz == nb:
                    self.free.pop(k)
                else:
                    self.free[k] = (o + nb, sz - nb)
                self.used += nb
                self.peak = max(self.peak, self.used)
                self.n += 1
                t = self.nc.alloc_sbuf_tensor_at("t%d" % self.n, list(shape), dt, offset=self.BASE + o)
                return (o, nb), t
        raise RuntimeError("arena OOM need %d free %s" % (nb, self.free))

    def release(self, o, nb):
        self.used -= nb
        self.free.append((o, nb))
        self.free.sort()
        m = []
        for (a, b) in self.free:
            if m and m[-1][0] + m[-1][1] == a:
                m[-1] = (m[-1][0], m[-1][1] + b)
            else:
                m.append((a, b))
        self.free = m


class Group:
    def __init__(self, arena):
        self.a = arena
        self.items = []

    def alloc(self, shape, dt):
        h, ap = self.a.alloc(shape, dt)
        self.items.append(h)
        return ap

    def close(self):
        for (o, nb) in self.items:
            self.a.release(o, nb)
        self.items = []


class Ring:
    def __init__(self, items):
        self.items = items
        self.i = 0

    def next(self):
        it = self.items[self.i % len(self.items)]
        self.i += 1
        return it


def build(Rp, Rs, dbg=False, limit=99):
    nc = bass.Bass("TRN2", target_bir_lowering=False)
    Tp = Rp // NCORES
    Sp, Ss = 128 * Rp, 128 * Rs

    def din(name, shape, dt=F32):
        return nc.dram_tensor(name, list(shape), dt, kind="ExternalInput").ap()

    def dscr(name, shape, dt):
        return nc.dram_tensor(name, list(shape), dt).ap()

    xp_all = din("xp_all", [Rp, 128, D])
    xp_own = din("xp_own", [Tp, 128, D])
    xs_all = din("xs_all", [Rs, 128, D])
    w_in = din("w_in", [D, 3232])
    w_fourier = din("w_fourier", [512, D])
    w_uq = din("w_uq", [384, 768])
    w_ukv = din("w_ukv", [256, 1024])
    w_attn = din("w_attn", [512, D])
    w_o = din("w_o", [D, D])
    w_gate = din("w_gate", [D, DFF])
    w_up = din("w_up", [D, DFF])
    w_down = din("w_down", [DFF, D])
    gq_b = din("gq_b", [128, 384])
    gkv_b = din("gkv_b", [128, 256])
    ln_b = din("ln_b", [128, 4, D])
    ident_d = din("ident", [128, 128], BF16)
    Ep_d = din("Ep", [128, Rp, 2 * Tp * (128 // Rp)], BF16)
    Es_d = din("Es", [128, Rs, 2 * 128], BF16)
    cht_d = din("cht", [128, 2, 256], BF16)
    crp_d = din("crp", [128, 2, 128], BF16)
    crs_d = din("crs", [128, 2, 128], BF16)
    ropeKp_d = din("ropeKp", [128, Rp, 64])
    ropeQp_d = din("ropeQp", [128, Tp, NH * 64])
    ropeQs_d = din("ropeQs", [128, Rs, NH * 64])
    ropeKs_d = din("ropeKs", [128, Rs, 64])
    yp = nc.dram_tensor("yp", [Tp, 128, D], F32, kind="ExternalOutput").ap()
    ys = nc.dram_tensor("ys", [Rs, 128, D], F32, kind="ExternalOutput").ap()
    NT = Tp + Rs
    h_s = dscr("h_s", [NT, 128, D], F32)

    seqs = []
    for (nm, R, T, K2, xa, xo, E_d, cr_d, rK, rQ, yout, hoff) in [
        ("p", Rp, Tp, Tp * (128 // Rp), xp_all, xp_own, Ep_d, crp_d, ropeKp_d, ropeQp_d, yp, 0),
        ("s", Rs, Rs, 128, xs_all, xs_all, Es_d, crs_d, ropeKs_d, ropeQs_d, ys, Tp),
    ]:
        S = 128 * R
        seqs.append(dict(nm=nm, R=R, T=T, K2=K2, S=S, xa=xa, xo=xo, E=E_d, cr=cr_d, rK=rK, rQ=rQ, y=yout, hoff=hoff,
                         kn=dscr("kn_" + nm, [NH, 64, S], BF16), kr=dscr("kr_" + nm, [32, S], BF16),
                         vs=dscr("vs_" + nm, [NH, 128, R // 4, 260], BF16)))

    SKIP = set(os.environ.get('KSKIP', '').split(','))
    P = Prog(nc)
    top = ExitStack()

    arena = Arena(nc, 206 * 1024)

    def sbt(es, name, shape, dt):
        return es.alloc(list(shape), dt)

    gtop = Group(arena)
    ps = top.enter_context(nc.psum_tensor("ps", [128, 8, 512], F32))
    PB = [Reg("ps%d" % i) for i in range(8)]

    def psbf(b):
        return ps[:, b, :].bitcast(BF16)

    ident = sbt(gtop, "ident", [128, 128], BF16)
    cht = sbt(gtop, "cht", [128, 2, 256], BF16)
    gqb = sbt(gtop, "gqb", [128, 384], F32)
    gkvb = sbt(gtop, "gkvb", [128, 256], F32)
    ones32 = sbt(gtop, "ones32", [128, 64], F32)
    SW = 1024
    stg = [sbt(gtop, "stg%d" % i, [128, SW], F32) for i in range(3)]
    R_const = Reg("const")
    R_stg = [Reg("stg%d" % i) for i in range(3)]
    stg_ring = Ring([0, 1, 2])
    for (dst, src) in [(ident, ident_d), (cht, cht_d), (gqb, gq_b), (gkvb, gkv_b)]:
        P.op("sp", ("dma_start", dict(out=dst[:], in_=src)), writes=[R_const], dma=P.uniq("cst"))
    P.op("pool", ("memset", dict(ap=ones32[:], constant=1.0)), writes=[R_const])

    cast_rr = Ring(["act", "dve"])
    dq_ring = Ring(["sp"])

    def cast(eng, out, in_, reads, writes):
        if eng == "act":
            P.op("act", ("copy", dict(out=out, in_=in_)), reads=reads, writes=writes)
        else:
            P.op(eng, ("tensor_copy", dict(out=out, in_=in_)), reads=reads, writes=writes)

    def load_w(dst_fn, src_rows_fn, nchunks, ncols, wreg):
        for c in range(nchunks):
            for lo in range(0, ncols, SW):
                hi = min(ncols, lo + SW)
                s = stg_ring.next()
                P.op(dq_ring.next(), ("dma_start", dict(out=stg[s][:, 0:hi - lo], in_=src_rows_fn(c)[:, lo:hi])),
                     writes=[R_stg[s]], dma="stg%d" % s)
                cast(cast_rr.next(), dst_fn(c)[:, lo:hi], stg[s][:, 0:hi - lo], [R_stg[s]], [wreg.new()])

    phase = [0]

    def stop():
        phase[0] += 1
        return phase[0] >= limit

    stopped = False
    for sq in seqs:
        nm, R, T, K2, S = sq["nm"], sq["R"], sq["T"], sq["K2"], sq["S"]
        kb = 128 // R if R < 128 else 1
        NG = R // 4
        g_vp, g_px, g_1a, g_qt, g_ft, g_at = [Group(arena) for _ in range(6)]
        R_FT = Reg("FT")
        R_QT = Reg("QT")

        p1 = g_1a
        Vp = sbt(g_vp, "Vp", [128, 8 * K2, R], BF16)
        R_Vp = Reg("Vp")
        NX = 3
        x32 = [sbt(g_px, "x32_%d" % i, [128, D], F32) for i in range(NX)]
        R_x32 = [Reg("x32") for _ in range(NX)]
        xbf = [sbt(g_px, "xbf_%d" % i, [128, D], BF16) for i in range(2)]
        R_xbf = [Reg("xbf") for _ in range(2)]
        xT = [sbt(g_px, "xT_%d" % i, [128, 8, 128], BF16) for i in range(2)]
        R_xT = [Reg("xT") for _ in range(2)]
        st = [sbt(g_px, "st_%d" % i, [128, 8], F32) for i in range(2)]
        R_st = [Reg("st") for _ in range(2)]
        junk = sbt(g_px, "junk", [128, 384], F32)
        R_junk = Reg("junk")
        rtmp = [sbt(g_px, "rtmp_%d" % i, [128, 64], F32) for i in range(2)]
        R_rtmp = [Reg("rtmp") for _ in range(2)]
        w_a = sbt(p1, "w_a", [128, 8, 800], BF16)
        wuk = sbt(p1, "wuk", [128, 2, NH, 64], BF16)
        wuv = sbt(p1, "wuv", [128, 2, NH, 64], BF16)
        R_w = MultiReg("w_p1a")
        Et = sbt(p1, "Et", [128, R, 2 * K2], BF16)
        ropeK = sbt(p1, "ropeK", [128, R, 64], F32)
        R_tab = Reg("tab")
        w_in_v = w_in.rearrange("(c p) n -> c p n", p=128)
        load_w(lambda c: w_a[:, c, 0:512], lambda c: w_in_v[c, :, 0:512], 8, 512, R_w)
        load_w(lambda c: w_a[:, c, 512:800], lambda c: w_in_v[c, :, 896:1184], 8, 288, R_w)
        w_ukv_v = w_ukv.rearrange("(c p) n -> c p n", p=128)
        for c in range(2):
            s = stg_ring.next()
            P.op("sp", ("dma_start", dict(out=stg[s][:, 0:1024], in_=w_ukv_v[c])), writes=[R_stg[s]], dma="stg%d" % s)
            sv = stg[s][:, 0:1024].rearrange("p (h t d) -> p h t d", h=NH, t=2)
            cast("act", wuk[:, c, :, :], sv[:, :, 0, :], [R_stg[s]], [R_w.new()])
            cast("dve", wuv[:, c, :, :], sv[:, :, 1, :], [R_stg[s]], [R_w.new()])
        P.op("sp", ("dma_start", dict(out=Et[:], in_=sq["E"])), writes=[R_tab], dma=P.uniq("cst"))
        P.op("sp", ("dma_start", dict(out=ropeK[:], in_=sq["rK"])), writes=[R_tab], dma=P.uniq("cst"))

        ubf = [sbt(p1, "ubf_%d" % i, [128, 512], BF16) for i in range(2)]
        R_ubf = [Reg("ubf") for _ in range(2)]
        kvn = [sbt(p1, "kvn_%d" % i, [128, 384], BF16) for i in range(2)]
        R_kvn = [Reg("kvn") for _ in range(2)]
        ckvT4 = [sbt(p1, "ckvT4_%d" % i, [128, 2, 512], BF16) for i in range(2)]
        R_ckvT4 = [Reg("ckvT4") for _ in range(2)]
        krt4 = [sbt(p1, "krt4_%d" % i, [128, 512], BF16) for i in range(2)]
        R_krt4 = [Reg("krt4") for _ in range(2)]
        kst = [sbt(p1, "kst_%d" % i, [128, 4, 512], BF16) for i in range(2)]
        R_kst = [Reg("kst") for _ in range(2)]
        vst = [sbt(p1, "vst_%d" % i, [128, NH, 4, 65], BF16) for i in range(2)]
        R_vst = [Reg("vst") for _ in range(2)]
        for i in range(2):
            P.op("pool", ("memset", dict(ap=kvn[i][:], constant=0.0)), writes=[R_kvn[i]])
            P.op("pool", ("memset", dict(ap=vst[i][:], constant=1.0)), writes=[R_vst[i]])
        R_scr = Reg("scr_" + nm)
        B_T, B_U, B_KV, B_T2, B_K, B_V, B_A0, B_A1 = range(8)
        psA = ps[:, B_A0:B_A0 + 2, :].rearrange("p b n -> p (b n)")

        def load_x(i, src):
            s = i % NX
            P.op("sp", ("dma_start", dict(out=x32[s][:], in_=src[i])), writes=[R_x32[s]], dma="x%d" % s)
            b = i % 2
            P.op("act", ("copy", dict(out=xbf[b][:], in_=x32[s][:])), reads=[R_x32[s]], writes=[R_xbf[b]])

        def make_xT(i):
            b = i % 2
            for c in range(8):
                P.op("pe", ("transpose", dict(out=psbf(B_T)[:, c * 128:(c + 1) * 128], in_=xbf[b][:, c * 128:(c + 1) * 128], identity=ident[:])),
                     reads=[R_xbf[b], R_const], writes=[PB[B_T]])
            P.op("dve", ("tensor_copy", dict(out=xT[b][:].rearrange("p c n -> p (c n)"), in_=psbf(B_T)[:, :])),
                 reads=[PB[B_T]], writes=[R_xT[b]])

        def rms_rstd(b, src_ps, n, breg):
            P.op("act", ("activation", dict(out=junk[:, 0:n], in_=src_ps, func=AF.Square, accum_out=st[b][:, 0:1])),
                 reads=[breg], writes=[R_junk, R_st[b]])
            P.op("act", ("activation", dict(out=st[b][:, 1:2], in_=st[b][:, 0:1], func=AF.Sqrt, scale=1.0 / n, bias=RMS_EPS)),
                 reads=[R_st[b]], writes=[R_st[b]])
            P.op("dve", ("reciprocal", dict(out=st[b][:, 2:3], in_=st[b][:, 1:2])), reads=[R_st[b]], writes=[R_st[b]])

        knv = sq["kn"].rearrange("(hp two) d s -> two d hp s", two=2)
        B_KV1 = 5
        B_VK = 4
        A_regs = [PB[B_A0], PB[B_A1]] if 8 * K2 > 512 else [PB[B_A0]]

        def bkv_of(i):
            return B_KV if i % 2 == 0 else B_KV1

        def s_front(i):
            b = i % 2
            bkv = bkv_of(i)
            make_xT(i)
            for c in range(8):
                P.op("pe", ("matmul", dict(out=ps[:, B_U, :], lhsT=xT[b][:, c, :], rhs=w_a[:, c, 0:512], start=(c == 0), stop=(c == 7))),
                     reads=[R_xT[b], R_w], writes=[PB[B_U]])
            for c in range(8):
                P.op("pe", ("matmul", dict(out=ps[:, bkv, 0:288], lhsT=xT[b][:, c, :], rhs=w_a[:, c, 512:800], start=(c == 0), stop=(c == 7))),
                     reads=[R_xT[b], R_w], writes=[PB[bkv]])

        def s_ubf(i):
            b = i % 2
            P.op("dve", ("tensor_copy", dict(out=ubf[b][:], in_=ps[:, B_U, :])), reads=[PB[B_U]], writes=[R_ubf[b]])

        def s_chain(i):
            b = i % 2
            bkv = bkv_of(i)
            rms_rstd(b, ps[:, bkv, 0:256], 256, PB[bkv])
            P.op("dve", ("scalar_tensor_tensor", dict(out=kvn[b][:, 0:256], in0=ps[:, bkv, 0:256], scalar=st[b][:, 2:3], in1=gkvb[:],
                                                      op0=ALU.mult, op1=ALU.mult)),
                 reads=[PB[bkv], R_st[b], R_const], writes=[R_kvn[b]])
            kr = ps[:, bkv, 256:288]
            P.op("dve", ("tensor_tensor", dict(out=rtmp[b][:, 0:32], in0=kr, in1=ropeK[:, i, 0:32], op=ALU.mult)),
                 reads=[PB[bkv], R_tab], writes=[R_rtmp[b]])
            P.op("dve", ("tensor_tensor", dict(out=rtmp[b][:, 32:48], in0=ps[:, bkv, 272:288], in1=ropeK[:, i, 32:48], op=ALU.mult)),
                 reads=[PB[bkv], R_tab], writes=[R_rtmp[b]])
            P.op("dve", ("tensor_tensor", dict(out=rtmp[b][:, 48:64], in0=ps[:, bkv, 256:272], in1=ropeK[:, i, 48:64], op=ALU.mult)),
                 reads=[PB[bkv], R_tab], writes=[R_rtmp[b]])
            P.op("dve", ("tensor_tensor", dict(out=kvn[b][:, 320:352], in0=rtmp[b][:, 0:32], in1=rtmp[b][:, 32:64], op=ALU.add)),
                 reads=[R_rtmp[b]], writes=[R_kvn[b]])

        def s_back(i):
            b = i % 2
            G, t4 = i // 4, i % 4
            gb = G % 2
            for g in range(4):
                P.op("pe", ("matmul", dict(out=psA[:, g * 2 * K2:(g + 1) * 2 * K2], lhsT=ubf[b][:, g * 128:(g + 1) * 128], rhs=Et[:, i, :],
                                           start=True, stop=True)),
                     reads=[R_ubf[b], R_tab], writes=A_regs)
            P.op("dve", ("tensor_copy", dict(out=Vp[:, :, i], in_=psA[:, 0:8 * K2])), reads=A_regs, writes=[R_Vp])
            for c in range(2):
                P.op("pe", ("transpose", dict(out=psbf(B_T2)[:, c * 128:(c + 1) * 128], in_=kvn[b][:, c * 128:(c + 1) * 128], identity=ident[:])),
                     reads=[R_kvn[b], R_const], writes=[PB[B_T2]])
            P.op("pe", ("transpose", dict(out=psbf(B_T2)[:, 256:384], in_=kvn[b][:, 256:384], identity=ident[:])),
                 reads=[R_kvn[b], R_const], writes=[PB[B_T2]])
            P.op("act", ("copy", dict(out=ckvT4[gb][:, :, t4 * 128:(t4 + 1) * 128], in_=psbf(B_T2)[:, 0:256].rearrange("p (c n) -> p c n", c=2))),
                 reads=[PB[B_T2]], writes=[R_ckvT4[gb]])
            P.op("act", ("copy", dict(out=krt4[gb][64:96, t4 * 128:(t4 + 1) * 128], in_=psbf(B_T2)[64:96, 256:384])),
                 reads=[PB[B_T2]], writes=[R_krt4[gb]])
            for c in range(2):
                P.op("pe", ("matmul", dict(out=ps[:, B_VK, :], lhsT=ckvT4[gb][:, c, t4 * 128:(t4 + 1) * 128],
                                           rhs=wuv[:, c, :, :].rearrange("p h d -> p (h d)"), start=(c == 0), stop=(c == 1))),
                     reads=[R_ckvT4[gb], R_w], writes=[PB[B_VK]])
            P.op("act", ("copy", dict(out=vst[gb][:, :, t4, 0:64], in_=ps[:, B_VK, :].rearrange("p (h d) -> p h d", h=NH))),
                 reads=[PB[B_VK]], writes=[R_vst[gb]])
            if t4 == 3:
                cols = slice(G * 512, (G + 1) * 512)
                for hp in range(4):
                    kbk = B_VK if (hp % 2 == 0 or 8 * K2 > 512) else B_A1
                    for c in range(2):
                        P.op("pe", ("matmul", dict(out=ps[:, kbk, :], lhsT=wuk[:, c, 2 * hp:2 * hp + 2, :].rearrange("p h d -> p (h d)"),
                                                   rhs=ckvT4[gb][:, c, :], start=(c == 0), stop=(c == 1))),
                             reads=[R_ckvT4[gb], R_w], writes=[PB[kbk]])
                    if hp % 2 == 0:
                        P.op("dve", ("tensor_copy", dict(out=kst[gb][:, hp, :], in_=ps[:, kbk, :])),
                             reads=[PB[kbk]], writes=[R_kst[gb]])
                    else:
                        P.op("act", ("copy", dict(out=kst[gb][:, hp, :], in_=ps[:, kbk, :])),
                             reads=[PB[kbk]], writes=[R_kst[gb]])
                for two in range(2):
                    P.op("pool", ("dma_start", dict(out=knv[two][:, :, cols], in_=kst[gb][two * 64:(two + 1) * 64, :, :])),
                         reads=[R_kst[gb]], writes=[R_scr], dma="kst%d" % gb)
                P.op("pool", ("dma_start", dict(out=sq["kr"][:, cols], in_=krt4[gb][64:96, :])),
                     reads=[R_krt4[gb]], writes=[R_scr], dma="kst%d" % gb)
                P.op("pool", ("dma_start", dict(out=sq["vs"][:, :, G, :].rearrange("h p n -> p h n"),
                                                in_=vst[gb][:].rearrange("p h t d -> p h (t d)"))),
                     reads=[R_vst[gb]], writes=[R_scr], dma="vst%d" % gb)

        load_x(0, sq["xa"])
        if R > 1:
            load_x(1, sq["xa"])
        s_front(0)
        s_ubf(0)
        for i in range(R):
            if i + 2 < R:
                load_x(i + 2, sq["xa"])
            if i + 1 < R:
                s_front(i + 1)
            s_chain(i)
            s_back(i)
            if i + 1 < R:
                s_ubf(i + 1)
        P.barrier()
        g_1a.close()
        if stop():
            stopped = True
            break

        p1b = Group(arena)
        QT = sbt(g_qt, "QT_" + nm, [128, NH, T * 128], BF16)
        w_q = sbt(p1b, "w_q", [128, 8, 384], BF16)
        wuq = sbt(p1b, "wuq", [128, 3, 768], BF16)
        ropeQ = [sbt(p1b, "ropeQ%d" % i, [128, 4, NH, 16], F32) for i in range(2)]
        R_ropeQ = [Reg("ropeQ") for _ in range(2)]
        load_w(lambda c: w_q[:, c, :], lambda c: w_in_v[c, :, 512:896], 8, 384, R_w)
        w_uq_v = w_uq.rearrange("(c p) n -> c p n", p=128)
        load_w(lambda c: wuq[:, c, :], lambda c: w_uq_v[c], 3, 768, R_w)
        cqn = [sbt(p1b, "cqn_%d" % i, [128, 384], BF16) for i in range(2)]
        R_cqn = [Reg("cqn") for _ in range(2)]
        cqT = [sbt(p1b, "cqT_%d" % i, [128, 3, 128], BF16) for i in range(2)]
        R_cqT = [Reg("cqT") for _ in range(2)]
        qbf = [sbt(p1b, "qbf_%d" % i, [128, NH, 96], BF16) for i in range(2)]
        R_qbf = [Reg("qbf") for _ in range(2)]
        qf = [sbt(p1b, "qf_%d" % i, [128, 768], F32) for i in range(2)]
        R_qf = [Reg("qf") for _ in range(2)]
        qtmp = [sbt(p1b, "qtmp_%d" % i, [128, NH, 64], F32) for i in range(2)]
        R_qtmp = [Reg("qtmp") for _ in range(2)]
        psQ = ps[:, B_KV:B_KV + 2, :].rearrange("p b n -> p (b n)")
        B_Q0, B_Q1 = B_KV, B_KV + 1
        CQB = [B_U, 4]

        def q_front(j):
            b = j % 2
            cb = CQB[j % 2]
            make_xT(j)
            for c in range(8):
                P.op("pe", ("matmul", dict(out=ps[:, cb, 0:384], lhsT=xT[b][:, c, :], rhs=w_q[:, c, :], start=(c == 0), stop=(c == 7))),
                     reads=[R_xT[b], R_w], writes=[PB[cb]])
            P.op("sp", ("dma_start", dict(out=ropeQ[b][:].rearrange("p a h d -> p (a h d)"), in_=sq["rQ"][:, j, :])),
                 writes=[R_ropeQ[b]], dma="rq%d" % b)

        def q_chain(j):
            b = j % 2
            cb = CQB[j % 2]
            rms_rstd(b, ps[:, cb, 0:384], 384, PB[cb])
            P.op("dve", ("scalar_tensor_tensor", dict(out=cqn[b][:], in0=ps[:, cb, 0:384], scalar=st[b][:, 2:3], in1=gqb[:],
                                                      op0=ALU.mult, op1=ALU.mult)),
                 reads=[PB[cb], R_st[b], R_const], writes=[R_cqn[b]])

        def q_back(j):
            b = j % 2
            for c in range(3):
                P.op("pe", ("transpose", dict(out=psbf(B_A0)[:, c * 128:(c + 1) * 128], in_=cqn[b][:, c * 128:(c + 1) * 128], identity=ident[:])),
                     reads=[R_cqn[b], R_const], writes=[PB[B_A0]])
            P.op("act", ("copy", dict(out=cqT[b][:].rearrange("p c n -> p (c n)"), in_=psbf(B_A0)[:, 0:384])), reads=[PB[B_A0]], writes=[R_cqT[b]])
            for (lo, hi, bank) in [(0, 512, B_Q0), (512, 768, B_Q1)]:
                for c in range(3):
                    P.op("pe", ("matmul", dict(out=psQ[:, lo:hi], lhsT=cqT[b][:, c, :], rhs=wuq[:, c, lo:hi], start=(c == 0), stop=(c == 2))),
                         reads=[R_cqT[b], R_w], writes=[PB[bank]])
            P.op("act", ("copy", dict(out=qf[b][:], in_=psQ[:, 0:768])), reads=[PB[B_Q0], PB[B_Q1]], writes=[R_qf[b]])
            q3 = qf[b][:].rearrange("p (h d) -> p h d", h=NH)
            rq = [R_qf[b]]
            P.op("dve", ("tensor_copy", dict(out=qbf[b][:, :, 0:64], in_=q3[:, :, 0:64])), reads=rq, writes=[R_qbf[b]])
            c1 = ropeQ[b][:, 0, :, :]
            c2 = ropeQ[b][:, 1, :, :]
            s1 = ropeQ[b][:, 2, :, :]
            s2 = ropeQ[b][:, 3, :, :]
            P.op("dve", ("tensor_tensor", dict(out=qtmp[b][:, :, 0:16], in0=q3[:, :, 64:80], in1=c1, op=ALU.mult)), reads=rq + [R_ropeQ[b]], writes=[R_qtmp[b]])
            P.op("dve", ("tensor_tensor", dict(out=qtmp[b][:, :, 16:32], in0=q3[:, :, 80:96], in1=c2, op=ALU.mult)), reads=rq + [R_ropeQ[b]], writes=[R_qtmp[b]])
            P.op("dve", ("tensor_tensor", dict(out=qtmp[b][:, :, 32:48], in0=q3[:, :, 80:96], in1=s1, op=ALU.mult)), reads=rq + [R_ropeQ[b]], writes=[R_qtmp[b]])
            P.op("dve", ("tensor_tensor", dict(out=qtmp[b][:, :, 48:64], in0=q3[:, :, 64:80], in1=s2, op=ALU.mult)), reads=rq + [R_ropeQ[b]], writes=[R_qtmp[b]])
            P.op("dve", ("tensor_tensor", dict(out=qbf[b][:, :, 64:96], in0=qtmp[b][:, :, 0:32], in1=qtmp[b][:, :, 32:64], op=ALU.add)),
                 reads=[R_qtmp[b]], writes=[R_qbf[b]])
            for h in range(NH):
                P.op("pe", ("transpose", dict(out=psbf(B_V)[0:96, h * 128:(h + 1) * 128], in_=qbf[b][:, h, :], identity=ident[:])),
                     reads=[R_qbf[b], R_const], writes=[PB[B_V]])
            P.op("act", ("copy", dict(out=QT[0:96, :, j * 128:(j + 1) * 128], in_=psbf(B_V)[0:96, :].rearrange("p (h n) -> p h n", h=NH))),
                 reads=[PB[B_V]], writes=[R_QT])

        load_x(0, sq["xo"])
        if T > 1:
            load_x(1, sq["xo"])
        q_front(0)
        for j in range(T):
            if j + 2 < T:
                load_x(j + 2, sq["xo"])
            if j + 1 < T:
                q_front(j + 1)
            q_chain(j)
            q_back(j)
        P.barrier()
        p1b.close()
        g_px.close()
        if stop():
            stopped = True
            break

        p2 = Group(arena)
        FT = sbt(g_ft, "FT_" + nm, [128, 4, T * 128], BF16)
        crt = sbt(p2, "crt", [128, 2, 128], BF16)
        P.op("sp", ("dma_start", dict(out=crt[:], in_=sq["cr"])), writes=[R_tab], dma=P.uniq("cst"))
        Zb = [sbt(p2, "Zb_%d" % i, [128, 256], BF16) for i in range(2)]
        R_Zb = [Reg("Zb") for _ in range(2)]
        Vp5 = Vp[:].rearrange("p (g ri k) i -> p g ri k i", g=4, ri=2)
        FTv = FT[:].rearrange("q g (i k1 pk) -> q g i k1 pk", k1=R, pk=kb)
        zi = 0
        for bI in range(K2 // kb):
            yb = 2 + (bI % 2)
            for g in range(4):
                zb = zi % 2
                zi += 1
                for ri in range(2):
                    P.op("pe", ("matmul", dict(out=ps[:, zb, 0:256], lhsT=Vp5[:, g, ri, bI * kb:(bI + 1) * kb, :].rearrange("p k i -> p (k i)"),
                                                             rhs=cht[:, ri, :], start=(ri == 0), stop=(ri == 1))),
                         reads=[R_Vp, R_const], writes=[PB[zb]])
                if zb == 0:
                    P.op("act", ("copy", dict(out=Zb[zb][:], in_=ps[:, zb, 0:256])), reads=[PB[zb]], writes=[R_Zb[zb]])
                else:
                    P.op("dve", ("tensor_copy", dict(out=Zb[zb][:], in_=ps[:, zb, 0:256])), reads=[PB[zb]], writes=[R_Zb[zb]])
                for ri in range(2):
                    P.op("pe", ("matmul", dict(out=ps[:, yb, g * 128:(g + 1) * 128], lhsT=Zb[zb][:, ri * 128:(ri + 1) * 128], rhs=crt[:, ri, :],
                                               start=(ri == 0), stop=(ri == 1))),
                         reads=[R_Zb[zb], R_tab], writes=[PB[yb]])
            i0, p0 = (kb * bI) % T, (kb * bI) // T
            P.op("act", ("copy", dict(out=FTv[:, :, i0:i0 + kb, :, p0],
                                                                    in_=ps[:, yb, 0:4 * kb * R].rearrange("p (g k r) -> p g k r", g=4, k=kb))),
                 reads=[PB[yb]], writes=[R_FT])
        P.barrier()
        p2.close()
        g_vp.close()
        if stop():
            stopped = True
            break

        p3 = Group(arena)
        attnT = sbt(g_at, "attnT_" + nm, [128, 4, T * 128], BF16)
        R_attnT = Reg("attnT")
        NS = 3
        kTt = [sbt(p3, "kTt_%d" % i, [128, 512], BF16) for i in range(NS)]
        Vt = [sbt(p3, "Vt_%d" % i, [128, 4, 128], BF16) for i in range(NS)]
        R_kv = [Reg("kvslot") for _ in range(NS)]
        for i in range(NS):
            P.op("pool", ("memset", dict(ap=Vt[i][:], constant=1.0)), writes=[R_kv[i]])
        pT = [sbt(p3, "pT_%d" % i, [128, 1024], BF16) for i in range(3)]
        R_pT = [Reg("pT") for _ in range(3)]
        rs = sbt(p3, "rs", [128, 512], F32)
        R_rs = Reg("rs")
        osb = [sbt(p3, "osb_%d" % i, [128, 512], F32) for i in range(2)]
        R_osb = [Reg("osb") for _ in range(2)]
        Nown = T * 128
        QW = min(512, Nown)
        nqc = Nown // QW
        qgroups = [list(range(a, min(a + 4, nqc))) for a in range(0, nqc, 4)]
        nkc = S // 512
        ci_box = [0]
        ni = 0
        for h in range(NH):
            for qg in qgroups:
                units = []
                a = 0
                while a < len(qg):
                    if a + 1 < len(qg):
                        units.append([qg[a], qg[a + 1]])
                        a += 2
                    else:
                        units.append([qg[a]])
                        a += 1
                steps = []
                for kc in range(nkc):
                    for kt in range(4):
                        for u in units:
                            steps.append((kc, kt, u))
                slot_of = {}

                def emit_qk(step, sb_):
                    nonlocal_ci = None
                    kc, kt, u = step
                    if kc not in slot_of:
                        s = ci_box[0] % NS
                        ci_box[0] += 1
                        slot_of[kc] = s
                        cols = slice(kc * 512, (kc + 1) * 512)
                        P.op("sp", ("dma_start", dict(out=kTt[s][0:64, :], in_=sq["kn"][h][:, cols])),
                             reads=[R_scr], writes=[R_kv[s]], dma="kv%d" % s)
                        P.op("sp", ("dma_start", dict(out=kTt[s][64:96, :], in_=sq["kr"][:, cols])),
                             reads=[R_scr], writes=[R_kv[s]], dma="kv%d" % s)
                        P.op("sp", ("dma_start", dict(out=Vt[s][:, :, 0:65], in_=sq["vs"][h, :, kc, :].rearrange("p (t d) -> p t d", t=4))),
                             reads=[R_scr], writes=[R_kv[s]], dma="kv%d" % s)
                    s = slot_of[kc]
                    banks = [4 + 2 * sb_, 5 + 2 * sb_]
                    for ui, qc in enumerate(u):
                        P.op("pe", ("matmul", dict(out=ps[:, 4 + 2 * sb_ + ui, 0:QW], lhsT=kTt[s][0:96, kt * 128:(kt + 1) * 128],
                            rhs=QT[0:96, h, qc * QW:(qc + 1) * QW], start=True, stop=True)),
                            reads=[R_kv[s], R_QT], writes=[PB[banks[ui]]])

                def emit_exp(step, sb_, pb_):
                    kc, kt, u = step
                    banks = [4 + 2 * sb_, 5 + 2 * sb_]
                    nb = len(u)
                    if nb == 2 and QW == 512:
                        src = ps[:, 4 + 2 * sb_:6 + 2 * sb_, :].rearrange("p b n -> p (b n)")
                        dst = pT[pb_][:, :]
                    else:
                        src = ps[:, 4 + 2 * sb_:4 + 2 * sb_ + nb, 0:QW]
                        dst = pT[pb_][:, :].rearrange("p (b n) -> p b n", b=2)[:, 0:nb, 0:QW]
                    P.op("act", ("activation", dict(out=dst, in_=src, func=AF.Exp, scale=SCALE)),
                         reads=[PB[bk] for bk in banks[:nb]], writes=[R_pT[pb_]])

                def emit_pv(step, pb_):
                    kc, kt, u = step
                    s = slot_of[kc]
                    for ui, qc in enumerate(u):
                        acc = qg.index(qc)
                        P.op("pe", ("matmul", dict(out=ps[:, acc, 0:QW], lhsT=Vt[s][:, kt, :], rhs=pT[pb_][:, ui * 512:ui * 512 + QW],
                            start=(kc == 0 and kt == 0), stop=(kc == nkc - 1 and kt == 3))),
                            reads=[R_kv[s], R_pT[pb_]], writes=[PB[acc]])

                emit_qk(steps[0], 0)
                if len(steps) > 1:
                    emit_qk(steps[1], 1)
                for ii, stp in enumerate(steps):
                    emit_exp(stp, ii % 2, ii % 3)
                    if ii + 2 < len(steps):
                        emit_qk(steps[ii + 2], ii % 2)
                    emit_pv(stp, ii % 3)
                for qc in qg:
                    acc = qg.index(qc)
                    nb_ = ni % 2
                    ni += 1
                    bcb = 4 + 2 * nb_
                    po = (h % 2) * 64
                    P.op("dve", ("reciprocal", dict(out=osb[nb_][64:128, 0:QW], in_=ps[64:128, acc, 0:QW])), reads=[PB[acc]], writes=[R_osb[nb_]])
                    P.op("dve", ("tensor_tensor", dict(
                        out=attnT[po:po + 64, h // 2, qc * QW:(qc + 1) * QW], in0=ps[0:64, acc, 0:QW], in1=osb[nb_][64:128, 0:QW], op=ALU.mult)),
                        reads=[R_osb[nb_], PB[acc]], writes=[R_attnT])
        P.barrier()
        p3.close()
        g_qt.close()
        if stop():
            stopped = True
            break

        p4 = Group(arena)
        lnb = sbt(p4, "lnb", [128, 2, D], F32)
        P.op("sp", ("dma_start", dict(out=lnb[:], in_=ln_b[:, 0:2, :])), writes=[R_const], dma=P.uniq("cst"))
        wg = sbt(p4, "wg", [128, 8, 2048], BF16)
        wf = sbt(p4, "wf", [128, 4, D], BF16)
        wat = sbt(p4, "wat", [128, 4, D], BF16)
        wo = sbt(p4, "wo", [128, 8, D], BF16)
        R_w4 = MultiReg("w4")
        load_w(lambda c: wg[:, c, :], lambda c: w_in_v[c, :, 1184:3232], 8, 2048, R_w4)
        wfv = w_fourier.rearrange("(c p) n -> c p n", p=128)
        load_w(lambda c: wf[:, c, :], lambda c: wfv[c], 4, D, R_w4)
        wav = w_attn.rearrange("(c p) n -> c p n", p=128)
        load_w(lambda c: wat[:, c, :], lambda c: wav[c], 4, D, R_w4)
        wov = w_o.rearrange("(c p) n -> c p n", p=128)
        load_w(lambda c: wo[:, c, :], lambda c: wov[c], 8, D, R_w4)
        BT = min(2, T)
        W = BT * 128
        nblk = T // BT
        x4 = [sbt(p4, "x4_%d" % i, [128, BT, D], F32) for i in range(2)]
        R_x4 = [Reg("x4") for _ in range(2)]
        xb4 = sbt(p4, "xb4", [128, BT, D], BF16)
        R_xb4 = Reg("xb4")
        xT4 = sbt(p4, "xT4", [128, BT, 8, 128], BF16)
        R_xT4 = Reg("xT4")
        mT = sbt(p4, "mT", [128, 8, W], BF16)
        R_mT = Reg("mT")
        sg = [sbt(p4, "sg_%d" % i, [128, 2, W], F32) for i in range(2)]
        R_sg = [Reg("sg") for _ in range(2)]
        mt = [sbt(p4, "mt_%d" % i, [128, 2, W], F32) for i in range(2)]
        R_mt = [Reg("mt") for _ in range(2)]
        rr = [sbt(p4, "rr_%d" % i, [128, D], F32) for i in range(3)]
        R_rr = [Reg("rr") for _ in range(3)]
        lst = [sbt(p4, "lst_%d" % i, [128, 24], F32) for i in range(3)]
        R_lst = [Reg("lst") for _ in range(3)]
        R_hs = Reg("h_s")

        def layer_norm(b, src, R_src, dst, R_dst, gi):
            L = lst[b]
            for k in range(2):
                P.op("dve", ("bn_stats", dict(out=L[:, k * 6:(k + 1) * 6], in_=src[:, k * 512:(k + 1) * 512])), reads=[R_src], writes=[R_lst[b]])
            P.op("dve", ("bn_aggr", dict(out=L[:, 12:14], in_=L[:, 0:12].rearrange("p (k s) -> p k s", k=2))), reads=[R_lst[b]], writes=[R_lst[b]])
            P.op("act", ("activation", dict(out=L[:, 14:15], in_=L[:, 13:14], func=AF.Sqrt, scale=1.0, bias=LN_EPS)), reads=[R_lst[b]], writes=[R_lst[b]])
            P.op("dve", ("reciprocal", dict(out=L[:, 15:16], in_=L[:, 14:15])), reads=[R_lst[b]], writes=[R_lst[b]])
            P.op("dve", ("tensor_scalar", dict(out=L[:, 16:17], in0=L[:, 12:13], scalar1=L[:, 15:16], scalar2=-1.0, op0=ALU.mult, op1=ALU.mult)),
                 reads=[R_lst[b]], writes=[R_lst[b]])
            P.op("act", ("activation", dict(out=dst, in_=src, func=AF.Identity, scale=L[:, 15:16], bias=L[:, 16:17])),
                 reads=[R_src, R_lst[b]], writes=[R_dst])
            P.op("dve", ("tensor_tensor", dict(out=dst, in0=dst, in1=lnb[:, gi, :], op=ALU.mult)), reads=[R_dst, R_const], writes=[R_dst])
            P.op("dve", ("tensor_tensor", dict(out=dst, in0=dst, in1=lnb[:, gi + 1, :], op=ALU.add)), reads=[R_dst, R_const], writes=[R_dst])

        def load_x4(bi):
            xb = bi % 2
            P.op("sp", ("dma_start", dict(out=x4[xb][:], in_=sq["xo"][bi * BT:(bi + 1) * BT].rearrange("t p d -> p t d"))),
                 writes=[R_x4[xb]], dma="x4_%d" % xb)

        ti_box = [0]

        def p4_cast(bi):
            xb = bi % 2
            for t in range(BT):
                P.op("dve", ("tensor_copy", dict(out=xb4[:, t, :], in_=x4[xb][:, t, :])), reads=[R_x4[xb]], writes=[R_xb4])

        def p4_tr(bi):
            for t in range(BT):
                for c in range(8):
                    P.op("pe", ("transpose", dict(out=psbf(t % 2)[:, c * 128:(c + 1) * 128], in_=xb4[:, t, c * 128:(c + 1) * 128], identity=ident[:])),
                         reads=[R_xb4, R_const], writes=[PB[t % 2]])
                P.op("dve", ("tensor_copy", dict(out=xT4[:, t, :, :].rearrange("p c n -> p (c n)"), in_=psbf(t % 2)[:, :])),
                     reads=[PB[t % 2]], writes=[R_xT4])

        def p4_fc(bi):
            cols = slice(bi * W, (bi + 1) * W)
            for fc in range(8):
                pb = 4 * (fc % 2)
                fs = slice(fc * 128, (fc + 1) * 128)
                sb2 = fc % 2
                for gi in range(2):
                    for c in range(8):
                        P.op("pe", ("matmul", dict(out=ps[:, pb + gi, 0:W], lhsT=wg[:, c, gi * 1024 + fc * 128:gi * 1024 + (fc + 1) * 128],
                                                   rhs=xT4[:, :, c, :], start=(c == 0), stop=(c == 7))),
                             reads=[R_w4, R_xT4], writes=[PB[pb + gi]])
                for c in range(4):
                    P.op("pe", ("matmul", dict(out=ps[:, pb + 2, 0:W], lhsT=wf[:, c, fs], rhs=FT[:, c, cols], start=(c == 0), stop=(c == 3))),
                         reads=[R_w4, R_FT], writes=[PB[pb + 2]])
                for c in range(4):
                    P.op("pe", ("matmul", dict(out=ps[:, pb + 3, 0:W], lhsT=wat[:, c, fs], rhs=attnT[:, c, cols], start=(c == 0), stop=(c == 3))),
                         reads=[R_w4, R_attnT], writes=[PB[pb + 3]])
                for gi in range(2):
                    P.op("act", ("activation", dict(out=sg[sb2][:, gi, :], in_=ps[:, pb + gi, 0:W], func=AF.Sigmoid)),
                         reads=[PB[pb + gi]], writes=[R_sg[sb2]])
                for gi in range(2):
                    P.op("dve", ("tensor_tensor", dict(out=mt[sb2][:, gi, :], in0=sg[sb2][:, gi, :], in1=ps[:, pb + 2 + gi, 0:W], op=ALU.mult)),
                         reads=[R_sg[sb2], PB[pb + 2 + gi]], writes=[R_mt[sb2]])
                P.op("pool", ("tensor_tensor", dict(out=mT[:, fc, :], in0=mt[sb2][:, 0, :], in1=mt[sb2][:, 1, :], op=ALU.add)),
                     reads=[R_mt[sb2]], writes=[R_mT])

        def p4_wo(bi):
            xb = bi % 2
            tbs = []
            for t in range(BT):
                ti = ti_box[0]
                ti_box[0] += 1
                tb = ti % 3
                b0 = 4 * (ti % 2)
                tbs.append(tb)
                for half in range(2):
                    for c in range(8):
                        P.op("pe", ("matmul", dict(out=ps[:, b0 + half, :], lhsT=mT[:, c, t * 128:(t + 1) * 128],
                                                   rhs=wo[:, c, half * 512:(half + 1) * 512], start=(c == 0), stop=(c == 7))),
                             reads=[R_mT, R_w4], writes=[PB[b0 + half]])
                for half in range(2):
                    hsl = slice(half * 512, (half + 1) * 512)
                    P.op("dve", ("scalar_tensor_tensor", dict(out=rr[tb][:, hsl], in0=x4[xb][:, t, hsl], scalar=ALPHA, in1=ps[:, b0 + half, :],
                                                              op0=ALU.mult, op1=ALU.add)),
                         reads=[R_x4[xb], PB[b0 + half]], writes=[R_rr[tb]])
            return tbs

        def p4_ln(bi, tbs):
            for t, tb in enumerate(tbs):
                layer_norm(tb, rr[tb][:], R_rr[tb], rr[tb][:], R_rr[tb], 0)
                tg = sq["hoff"] + bi * BT + t
                P.op("pool", ("dma_start", dict(out=h_s[tg], in_=rr[tb][:])), reads=[R_rr[tb]], writes=[R_hs], dma="hst%d" % tb)

        load_x4(0)
        if nblk > 1:
            load_x4(1)
        p4_cast(0)
        p4_tr(0)
        for bi in range(nblk):
            p4_fc(bi)
            if bi + 1 < nblk:
                p4_cast(bi + 1)
            tbs = p4_wo(bi)
            if bi + 2 < nblk:
                load_x4(bi + 2)
            if bi + 1 < nblk:
                p4_tr(bi + 1)
            p4_ln(bi, tbs)
        P.barrier()
        p4.close()
        g_ft.close()
        g_at.close()
        if stop():
            stopped = True
            break

    if stopped:
        arena.free = [(0, 206 * 1024)]
        P.emit()
        return nc
    pf = Group(arena)
    lnb = sbt(pf, "lnb2", [128, 2, D], F32)
    P.op("sp", ("dma_start", dict(out=lnb[:], in_=ln_b[:, 2:4, :])), writes=[R_const], dma=P.uniq("cst"))
    wgt = sbt(pf, "wgt", [128, 8, DFF], BF16)
    wup = sbt(pf, "wup", [128, 8, DFF], BF16)
    wdn = sbt(pf, "wdn", [128, NF, D], BF16)
    R_wf = MultiReg("wffn")
    wgv = w_gate.rearrange("(c p) n -> c p n", p=128)
    wuv_ = w_up.rearrange("(c p) n -> c p n", p=128)
    wdv = w_down.rearrange("(c p) n -> c p n", p=128)
    load_w(lambda c: wgt[:, c, :], lambda c: wgv[c], 8, DFF, R_wf)
    load_w(lambda c: wup[:, c, :], lambda c: wuv_[c], 8, DFF, R_wf)
    load_w(lambda c: wdn[:, c, :], lambda c: wdv[c], NF, D, R_wf)
    FB = 2
    FW = FB * 128
    h4 = [sbt(pf, "h4_%d" % i, [128, FB, D], F32) for i in range(2)]
    R_h4 = [Reg("h4") for _ in range(2)]
    hb4 = sbt(pf, "hb4", [128, FB, D], BF16)
    R_hb4 = Reg("hb4")
    hT = sbt(pf, "hT", [128, FB, 8, 128], BF16)
    R_hT = Reg("hT")
    actT = sbt(pf, "actT", [128, NF, FW], BF16)
    R_actT = Reg("actT")
    sgl = [sbt(pf, "sgl_%d" % i, [128, FW], F32) for i in range(2)]
    R_sgl = [Reg("sgl") for _ in range(2)]
    r2 = [sbt(pf, "r2_%d" % i, [128, D], F32) for i in range(2)]
    R_r2 = [Reg("r2") for _ in range(2)]
    lst2 = [sbt(pf, "lst2_%d" % i, [128, 24], F32) for i in range(2)]
    R_lst2 = [Reg("lst2") for _ in range(2)]
    R_yout = Reg("yout")

    def layer_norm2(b, src, R_src, dst, R_dst, gi):
        L = lst2[b]
        ops = []
        for k in range(2):
            ops.append(lambda k=k: P.op("dve", ("bn_stats", dict(out=L[:, k * 6:(k + 1) * 6], in_=src[:, k * 512:(k + 1) * 512])), reads=[R_src], writes=[R_lst2[b]]))
        ops.append(lambda: P.op("dve", ("bn_aggr", dict(out=L[:, 12:14], in_=L[:, 0:12].rearrange("p (k s) -> p k s", k=2))), reads=[R_lst2[b]], writes=[R_lst2[b]]))
        ops.append(lambda: P.op("act", ("activation", dict(out=L[:, 14:15], in_=L[:, 13:14], func=AF.Sqrt, scale=1.0, bias=LN_EPS)), reads=[R_lst2[b]], writes=[R_lst2[b]]))
        ops.append(lambda: P.op("dve", ("reciprocal", dict(out=L[:, 15:16], in_=L[:, 14:15])), reads=[R_lst2[b]], writes=[R_lst2[b]]))
        ops.append(lambda: P.op("dve", ("tensor_scalar", dict(out=L[:, 16:17], in0=L[:, 12:13], scalar1=L[:, 15:16], scalar2=-1.0, op0=ALU.mult, op1=ALU.mult)),
                                reads=[R_lst2[b]], writes=[R_lst2[b]]))
        ops.append(lambda: P.op("act", ("activation", dict(out=dst, in_=src, func=AF.Identity, scale=L[:, 15:16], bias=L[:, 16:17])),
                                reads=[R_src, R_lst2[b]], writes=[R_dst]))
        ops.append(lambda: P.op("dve", ("tensor_tensor", dict(out=dst, in0=dst, in1=lnb[:, gi, :], op=ALU.mult)), reads=[R_dst, R_const], writes=[R_dst]))
        ops.append(lambda: P.op("dve", ("tensor_tensor", dict(out=dst, in0=dst, in1=lnb[:, gi + 1, :], op=ALU.add)), reads=[R_dst, R_const], writes=[R_dst]))
        return ops

    nfb = NT // FB

    def load_h(bi):
        hb = bi % 2
        P.op("sp", ("dma_start", dict(out=h4[hb][:], in_=h_s[bi * FB:(bi + 1) * FB].rearrange("t p d -> p t d"))),
             reads=[R_hs], writes=[R_h4[hb]], dma="h4_%d" % hb)

    cnt = {"gi": 0, "di": 0}

    def f_cast(bi):
        hb = bi % 2
        for t in range(FB):
            P.op("dve", ("tensor_copy", dict(out=hb4[:, t, :], in_=h4[hb][:, t, :])), reads=[R_h4[hb]], writes=[R_hb4])

    def f_tr(bi):
        for t in range(FB):
            for c in range(8):
                P.op("pe", ("transpose", dict(out=psbf(t % 2)[:, c * 128:(c + 1) * 128], in_=hb4[:, t, c * 128:(c + 1) * 128], identity=ident[:])),
                     reads=[R_hb4, R_const], writes=[PB[t % 2]])
            P.op("dve", ("tensor_copy", dict(out=hT[:, t, :, :].rearrange("p c n -> p (c n)"), in_=psbf(t % 2)[:, :])),
                 reads=[PB[t % 2]], writes=[R_hT])

    def f_gu(bi):
        for f in range(NF):
            gb_ = 2 + (cnt["gi"] % 2)
            sb3 = cnt["gi"] % 2
            cnt["gi"] += 1
            for (wm, off) in [(wgt, 0), (wup, FW)]:
                for c in range(8):
                    P.op("pe", ("matmul", dict(out=ps[:, gb_, off:off + FW], lhsT=wm[:, c, f * 128:(f + 1) * 128], rhs=hT[:, :, c, :],
                                               start=(c == 0), stop=(c == 7))),
                         reads=[R_wf, R_hT], writes=[PB[gb_]])
            P.op("act", ("activation", dict(out=sgl[sb3][:], in_=ps[:, gb_, 0:FW], func=AF.Silu)), reads=[PB[gb_]], writes=[R_sgl[sb3]])
            P.op("dve", ("tensor_tensor", dict(out=actT[:, f, :], in0=sgl[sb3][:], in1=ps[:, gb_, FW:2 * FW], op=ALU.mult)),
                 reads=[R_sgl[sb3], PB[gb_]], writes=[R_actT])
            if f >= 2 and pend:
                pend.pop(0)()

    def f_down(bi):
        hb = bi % 2
        dbs = []
        while pend:
            pend.pop(0)()
        for t in range(FB):
            db = cnt["di"] % 2
            cnt["di"] += 1
            dbs.append(db)
            b0 = 4 + 2 * db
            for half in range(2):
                for f in range(NF):
                    P.op("pe", ("matmul", dict(out=ps[:, b0 + half, :], lhsT=actT[:, f, t * 128:(t + 1) * 128],
                                               rhs=wdn[:, f, half * 512:(half + 1) * 512], start=(f == 0), stop=(f == NF - 1))),
                         reads=[R_actT, R_wf], writes=[PB[b0 + half]])
            for half in range(2):
                hsl = slice(half * 512, (half + 1) * 512)
                P.op("dve", ("scalar_tensor_tensor", dict(out=r2[db][:, hsl], in0=h4[hb][:, t, hsl], scalar=ALPHA, in1=ps[:, b0 + half, :],
                                                          op0=ALU.mult, op1=ALU.add)),
                     reads=[R_h4[hb], PB[b0 + half]], writes=[R_r2[db]])
        return dbs

    pend = []

    def f_ln(bi, dbs):
        for t, db in enumerate(dbs):
            pend.extend(layer_norm2(db, r2[db][:], R_r2[db], r2[db][:], R_r2[db], 0))
            tg = bi * FB + t
            dst = yp[tg] if tg < Tp else ys[tg - Tp]
            pend.append(lambda dst=dst, db=db: P.op("pool", ("dma_start", dict(out=dst, in_=r2[db][:])), reads=[R_r2[db]], writes=[R_yout], dma="yst%d" % db))

    load_h(0)
    if nfb > 1:
        load_h(1)
    f_cast(0)
    f_tr(0)
    for bi in range(nfb):
        f_gu(bi)
        if bi + 1 < nfb:
            f_cast(bi + 1)
        dbs = f_down(bi)
        if bi + 2 < nfb:
            load_h(bi + 2)
        if bi + 1 < nfb:
            f_tr(bi + 1)
        f_ln(bi, dbs)
    while pend:
        pend.pop(0)()
    P.emit()
    print("arena peak", arena.peak, flush=True)
    pf.close()
    top.close()
    return nc


def _bf(a):
    return np.asarray(a, dtype=np.float32).astype(ml_dtypes.bfloat16)


def _tile_order(x, R):
    S, d = x.shape
    return np.ascontiguousarray(x.reshape(128, R, d).transpose(1, 0, 2))


def _E_table(R, k2s):
    S = 128 * R
    i = np.arange(R)[None, :, None]
    p = np.arange(128)[:, None, None]
    n = i + R * p
    k2 = np.asarray(k2s)[None, None, :]
    m = (n * k2) % S
    th = 2 * np.pi * m / S
    s = 1.0 / np.sqrt(128.0)
    E = np.concatenate([np.cos(th) * s, -np.sin(th) * s], axis=2)
    return _bf(E)


def _ch_table():
    c = np.arange(128)
    th = 2 * np.pi * ((c[:, None] * c[None, :]) % 128) / 128
    s = 1.0 / np.sqrt(128.0)
    C, Sn = np.cos(th) * s, np.sin(th) * s
    T1 = np.concatenate([C, -Sn], axis=1)
    T2 = np.concatenate([Sn, C], axis=1)
    return _bf(np.stack([T1, T2], axis=1))


def _cr_table(R):
    q = np.arange(128)
    blk_r, n1 = q // R, q % R
    blk_c, k1 = q // R, q % R
    th = 2 * np.pi * ((n1[:, None] * k1[None, :]) % R) / R
    s = 1.0 / np.sqrt(float(R))
    mask = (blk_r[:, None] == blk_c[None, :]).astype(np.float64)
    return _bf(np.stack([np.cos(th) * s * mask, np.sin(th) * s * mask], axis=1))


def _rope_table(R, tiles):
    inv = (1.0 / (np.float32(10000.0) ** (np.arange(0, 32, 2, dtype=np.float32) / np.float32(32)))).astype(np.float32)
    pos = (np.asarray(tiles)[None, :] + R * np.arange(128)[:, None]).astype(np.float32)
    ang = (pos[:, :, None] * inv[None, None, :]).astype(np.float32)
    c, s = np.cos(ang).astype(np.float32), np.sin(ang).astype(np.float32)
    return np.ascontiguousarray(np.concatenate([c, c, -s, s], axis=2).astype(np.float32))


def _rope_rep(t):
    p, nt, _ = t.shape
    a = t.reshape(p, nt, 4, 1, 16)
    return np.ascontiguousarray(np.broadcast_to(a, (p, nt, 4, NH, 16)).reshape(p, nt, 4 * NH * 16)).astype(np.float32)


_NC_CACHE = {}


def run(inputs, Rp, Rs, dbg=False, limit=99):
    key = (Rp, Rs, limit)
    if key not in _NC_CACHE:
        _NC_CACHE[key] = build(Rp, Rs, limit=limit)
    nc = _NC_CACHE[key]
    Tp = Rp // NCORES
    f32 = lambda a: np.ascontiguousarray(np.asarray(a, dtype=np.float32))
    xp = _tile_order(f32(inputs["x_prompt"])[0], Rp)
    xs = f32(inputs["x_sample"])
    shared = dict(
        xp_all=xp,
        w_in=f32(inputs["w_in"])[0], w_fourier=f32(inputs["w_fourier"])[0], w_uq=f32(inputs["w_uq"])[0],
        w_ukv=f32(inputs["w_ukv"])[0], w_attn=f32(inputs["w_attn"])[0], w_o=f32(inputs["w_o"])[0],
        w_gate=f32(inputs["w_gate"])[0], w_up=f32(inputs["w_up"])[0], w_down=f32(inputs["w_down"])[0],
        gq_b=np.ascontiguousarray(np.broadcast_to(f32(inputs["g_q"])[0][None, :], (128, 384))),
        gkv_b=np.ascontiguousarray(np.broadcast_to(f32(inputs["g_kv"])[0][None, :], (128, 256))),
        ln_b=np.ascontiguousarray(np.broadcast_to(
            np.stack([f32(inputs[k])[0] for k in ["ln1_g", "ln1_b", "ln2_g", "ln2_b"]])[None], (128, 4, D))),
        ident=_bf(np.eye(128)),
        Es=_E_table(Rs, np.arange(128)),
        cht=_ch_table(), crp=_cr_table(Rp), crs=_cr_table(Rs),
        ropeKp=_rope_table(Rp, np.arange(Rp)), ropeKs=_rope_table(Rs, np.arange(Rs)),
    )
    ropeQs = _rope_rep(shared["ropeKs"])
    in_maps = []
    for r in range(NCORES):
        own = np.arange(r * Tp, (r + 1) * Tp)
        m = dict(shared)
        m["xp_own"] = np.ascontiguousarray(xp[own])
        m["xs_all"] = _tile_order(xs[r], Rs)
        kbp = 128 // Rp
        m["Ep"] = _E_table(Rp, np.concatenate([own + Rp * pk for pk in range(kbp)]))
        m["ropeQp"] = _rope_rep(shared["ropeKp"][:, own, :])
        m["ropeQs"] = ropeQs
        in_maps.append(m)
    res = run_bass_kernel_spmd(nc, in_maps, core_ids=list(range(NCORES)))
    yp = np.zeros((Rp, 128, D), np.float32)
    ys = np.zeros((NCORES, 128 * Rs, D), np.float32)
    for r in range(NCORES):
        o = res.results[r]
        yp[r * Tp:(r + 1) * Tp] = o["yp"]
        ys[r] = np.asarray(o["ys"]).transpose(1, 0, 2).reshape(128 * Rs, D)
    y_prompt = yp.transpose(1, 0, 2).reshape(1, 128 * Rp, D)
    return (np.ascontiguousarray(y_prompt), ys)


def kernel(**inputs):
    Rp = inputs["x_prompt"].shape[1] // 128
    Rs = inputs["x_sample"].shape[1] // 128
    return run(inputs, Rp, Rs)
```

```python
import os
import numpy as np
import ml_dtypes
from contextlib import ExitStack
import concourse.bass as bass
import concourse.mybir as mybir
from concourse.bass_utils import run_bass_kernel_spmd

F32 = mybir.dt.float32
BF16 = mybir.dt.bfloat16
ALU = mybir.AluOpType
AF = mybir.ActivationFunctionType

D = 1024
DFF = 2816
NF = 22
NH = 8
SCALE = 96 ** -0.5
ALPHA = 2.0 ** 0.25
LN_EPS = 1e-5
RMS_EPS = 1e-6
NCORES = 8


class Reg:
    __slots__ = ("name", "w", "r")

    def __init__(self, name):
        self.name = name
        self.w = None
        self.r = {}


class MultiReg:
    def __init__(self, name):
        self.name = name
        self.members = []

    def new(self):
        r = Reg(self.name)
        self.members.append(r)
        return r


def _flat(regs):
    out = []
    for r in regs:
        if isinstance(r, MultiReg):
            out.extend(r.members)
        else:
            out.append(r)
    return out


class Prog:
    ENGS = ["pe", "act", "dve", "pool", "sp"]

    def __init__(self, nc):
        self.nc = nc
        self.ops = {e: [] for e in self.ENGS}
        self.cnt = {}
        self.waited = {e: {} for e in self.ENGS}
        self.nwaits = 0
        self.epoch = 0

    def op(self, eng, fn, reads=(), writes=(), dma=None):
        reads = _flat(reads)
        deps = {}

        def add(dep):
            if dep is None:
                return
            k, v = dep
            if deps.get(k, 0) < v:
                deps[k] = v

        for r in reads:
            add(r.w)
        for w in writes:
            add(w.w)
            for d in w.r.items():
                add(d)
        key = dma if dma else "E_%s_%d" % (eng, self.epoch)
        inc = 16 if dma else 1
        val = self.cnt.get(key, 0) + inc
        self.cnt[key] = val
        waits = []
        wd = self.waited[eng]
        for k, v in deps.items():
            if eng == "pe" and k.startswith("E_pe_"):
                continue
            if wd.get(k, 0) < v:
                wd[k] = v
                waits.append((k, v))
        self.nwaits += len(waits)
        self.ops[eng].append((waits, fn, key, inc))
        for r in reads:
            if r.r.get(key, 0) < val:
                r.r[key] = val
        for w in writes:
            w.w = (key, val)
            w.r = {}
        return (key, val)

    def uniq(self, base):
        self._u = getattr(self, "_u", 0) + 1
        return "%s%d" % (base, self._u)

    def barrier(self):
        for e in self.ENGS:
            waits = []
            wd = self.waited[e]
            for k, v in self.cnt.items():
                if k.startswith("E_" + e + "_"):
                    continue
                if wd.get(k, 0) < v:
                    wd[k] = v
                    waits.append((k, v))
            if waits:
                self.ops[e].append((waits, None, None, 0))
        self.epoch += 1

    def emit(self):
        nc = self.nc
        self.barrier()
        with ExitStack() as es:
            sems = {k: es.enter_context(nc.semaphore(k)) for k in self.cnt}
            block = es.enter_context(nc.Block())

            def body(engname):
                def f(eng):
                    for waits, fn, key, inc in self.ops[engname]:
                        for (k, v) in waits:
                            eng.wait_ge(sems[k], v)
                        if fn is not None:
                            ins = getattr(eng, fn[0])(**fn[1])
                            ins.then_inc(sems[key], inc)
                return f

            block.tensor(body("pe"))
            block.scalar(body("act"))
            block.vector(body("dve"))
            block.gpsimd(body("pool"))
            block.sync(body("sp"))
        print("ops:", {e: len(v) for e, v in self.ops.items()}, "waits:", self.nwaits,
              "maxcnt:", max(self.cnt.values()), "nsems:", len(self.cnt), flush=True)


class Arena:
    BASE = 17408

    def __init__(self, nc, nbytes):
        self.nc = nc
        self.free = [(0, nbytes)]
        self.peak = 0
        self.used = 0
        self.n = 0

    def alloc(self, shape, dt):
        n = 1
        for d in shape[1:]:
            n *= d
        nb = n * (4 if dt == F32 else 2)
        nb = (nb + 255) // 256 * 256
        for k, (o, sz) in enumerate(self.free):
            if sz >= nb:
                if sz == nb:
                    self.free.pop(k)
                else:
                    self.free[k] = (o + nb, sz - nb)
                self.used += nb
                self.peak = max(self.peak, self.used)
                self.n += 1
                t = self.nc.alloc_sbuf_tensor_at("t%d" % self.n, list(shape), dt, offset=self.BASE + o)
                return (o, nb), t
        raise RuntimeError("arena OOM need %d free %s" % (nb, self.free))

    def release(self, o, nb):
        self.used -= nb
        self.free.append((o, nb))
        self.free.sort()
        m = []
        for (a, b) in self.free:
            if m and m[-1][0] + m[-1][1] == a:
                m[-1] = (m[-1][0], m[-1][1] + b)
            else:
                m.append((a, b))
        self.free = m


class Group:
    def __init__(self, arena):
        self.a = arena
        self.items = []

    def alloc(self, shape, dt):
        h, ap = self.a.alloc(shape, dt)
        self.items.append(h)
        return ap

    def close(self):
        for (o, nb) in self.items:
            self.a.release(o, nb)
        self.items = []


class Ring:
    def __init__(self, items):
        self.items = items
        self.i = 0

    def next(self):
        it = self.items[self.i % len(self.items)]
        self.i += 1
        return it


def build(Rp, Rs, dbg=False, limit=99):
    nc = bass.Bass("TRN2", target_bir_lowering=False)
    Tp = Rp // NCORES
    Sp, Ss = 128 * Rp, 128 * Rs

    def din(name, shape, dt=F32):
        return nc.dram_tensor(name, list(shape), dt, kind="ExternalInput").ap()

    def dscr(name, shape, dt):
        return nc.dram_tensor(name, list(shape), dt).ap()

    xp_all = din("xp_all", [Rp, 128, D])
    xp_own = din("xp_own", [Tp, 128, D])
    xs_all = din("xs_all", [Rs, 128, D])
    w_in = din("w_in", [D, 3232])
    w_fourier = din("w_fourier", [512, D])
    w_uq = din("w_uq", [384, 768])
    w_ukv = din("w_ukv", [256, 1024])
    w_attn = din("w_attn", [512, D])
    w_o = din("w_o", [D, D])
    w_gate = din("w_gate", [D, DFF])
    w_up = din("w_up", [D, DFF])
    w_down = din("w_down", [DFF, D])
    gq_b = din("gq_b", [128, 384])
    gkv_b = din("gkv_b", [128, 256])
    ln_b = din("ln_b", [128, 4, D])
    ident_d = din("ident", [128, 128], BF16)
    Ep_d = din("Ep", [128, Rp, 2 * Tp * (128 // Rp)], BF16)
    Es_d = din("Es", [128, Rs, 2 * 128], BF16)
    cht_d = din("cht", [128, 2, 256], BF16)
    crp_d = din("crp", [128, 2, 128], BF16)
    crs_d = din("crs", [128, 2, 128], BF16)
    ropeKp_d = din("ropeKp", [128, Rp, 64])
    ropeQp_d = din("ropeQp", [128, Tp, NH * 64])
    ropeQs_d = din("ropeQs", [128, Rs, NH * 64])
    ropeKs_d = din("ropeKs", [128, Rs, 64])
    yp = nc.dram_tensor("yp", [Tp, 128, D], F32, kind="ExternalOutput").ap()
    ys = nc.dram_tensor("ys", [Rs, 128, D], F32, kind="ExternalOutput").ap()
    NT = Tp + Rs
    h_s = dscr("h_s", [NT, 128, D], F32)

    seqs = []
    for (nm, R, T, K2, xa, xo, E_d, cr_d, rK, rQ, yout, hoff) in [
        ("p", Rp, Tp, Tp * (128 // Rp), xp_all, xp_own, Ep_d, crp_d, ropeKp_d, ropeQp_d, yp, 0),
        ("s", Rs, Rs, 128, xs_all, xs_all, Es_d, crs_d, ropeKs_d, ropeQs_d, ys, Tp),
    ]:
        S = 128 * R
        seqs.append(dict(nm=nm, R=R, T=T, K2=K2, S=S, xa=xa, xo=xo, E=E_d, cr=cr_d, rK=rK, rQ=rQ, y=yout, hoff=hoff,
                         kn=dscr("kn_" + nm, [NH, 64, S], BF16), kr=dscr("kr_" + nm, [32, S], BF16),
                         vs=dscr("vs_" + nm, [NH, 128, R // 4, 260], BF16)))

    SKIP = set(os.environ.get('KSKIP', '').split(','))
    P = Prog(nc)
    top = ExitStack()

    arena = Arena(nc, 206 * 1024)

    def sbt(es, name, shape, dt):
        return es.alloc(list(shape), dt)

    gtop = Group(arena)
    ps = top.enter_context(nc.psum_tensor("ps", [128, 8, 512], F32))
    PB = [Reg("ps%d" % i) for i in range(8)]

    def psbf(b):
        return ps[:, b, :].bitcast(BF16)

    ident = sbt(gtop, "ident", [128, 128], BF16)
    cht = sbt(gtop, "cht", [128, 2, 256], BF16)
    gqb = sbt(gtop, "gqb", [128, 384], F32)
    gkvb = sbt(gtop, "gkvb", [128, 256], F32)
    ones32 = sbt(gtop, "ones32", [128, 64], F32)
    SW = 1024
    stg = [sbt(gtop, "stg%d" % i, [128, SW], F32) for i in range(3)]
    R_const = Reg("const")
    R_stg = [Reg("stg%d" % i) for i in range(3)]
    stg_ring = Ring([0, 1, 2])
    for (dst, src) in [(ident, ident_d), (cht, cht_d), (gqb, gq_b), (gkvb, gkv_b)]:
        P.op("sp", ("dma_start", dict(out=dst[:], in_=src)), writes=[R_const], dma=P.uniq("cst"))
    P.op("pool", ("memset", dict(ap=ones32[:], constant=1.0)), writes=[R_const])

    cast_rr = Ring(["act", "dve"])
    dq_ring = Ring(["sp"])

    def cast(eng, out, in_, reads, writes):
        if eng == "act":
            P.op("act", ("copy", dict(out=out, in_=in_)), reads=reads, writes=writes)
        else:
            P.op(eng, ("tensor_copy", dict(out=out, in_=in_)), reads=reads, writes=writes)

    def load_w(dst_fn, src_rows_fn, nchunks, ncols, wreg):
        for c in range(nchunks):
            for lo in range(0, ncols, SW):
                hi = min(ncols, lo + SW)
                s = stg_ring.next()
                P.op(dq_ring.next(), ("dma_start", dict(out=stg[s][:, 0:hi - lo], in_=src_rows_fn(c)[:, lo:hi])),
                     writes=[R_stg[s]], dma="stg%d" % s)
                cast(cast_rr.next(), dst_fn(c)[:, lo:hi], stg[s][:, 0:hi - lo], [R_stg[s]], [wreg.new()])

    phase = [0]

    def stop():
        phase[0] += 1
        return phase[0] >= limit

    stopped = False
    for sq in seqs:
        nm, R, T, K2, S = sq["nm"], sq["R"], sq["T"], sq["K2"], sq["S"]
        kb = 128 // R if R < 128 else 1
        NG = R // 4
        g_vp, g_px, g_1a, g_qt, g_ft, g_at = [Group(arena) for _ in range(6)]
        R_FT = Reg("FT")
        R_QT = Reg("QT")

        p1 = g_1a
        Vp = sbt(g_vp, "Vp", [128, 8 * K2, R], BF16)
        R_Vp = Reg("Vp")
        NX = 3
        x32 = [sbt(g_px, "x32_%d" % i, [128, D], F32) for i in range(NX)]
        R_x32 = [Reg("x32") for _ in range(NX)]
        xbf = [sbt(g_px, "xbf_%d" % i, [128, D], BF16) for i in range(2)]
        R_xbf = [Reg("xbf") for _ in range(2)]
        xT = [sbt(g_px, "xT_%d" % i, [128, 8, 128], BF16) for i in range(2)]
        R_xT = [Reg("xT") for _ in range(2)]
        st = [sbt(g_px, "st_%d" % i, [128, 8], F32) for i in range(2)]
        R_st = [Reg("st") for _ in range(2)]
        junk = sbt(g_px, "junk", [128, 384], F32)
        R_junk = Reg("junk")
        rtmp = [sbt(g_px, "rtmp_%d" % i, [128, 64], F32) for i in range(2)]
        R_rtmp = [Reg("rtmp") for _ in range(2)]
        w_a = sbt(p1, "w_a", [128, 8, 800], BF16)
        wuk = sbt(p1, "wuk", [128, 2, NH, 64], BF16)
        wuv = sbt(p1, "wuv", [128, 2, NH, 64], BF16)
        R_w = MultiReg("w_p1a")
        Et = sbt(p1, "Et", [128, R, 2 * K2], BF16)
        ropeK = sbt(p1, "ropeK", [128, R, 64], F32)
        R_tab = Reg("tab")
        w_in_v = w_in.rearrange("(c p) n -> c p n", p=128)
        load_w(lambda c: w_a[:, c, 0:512], lambda c: w_in_v[c, :, 0:512], 8, 512, R_w)
        load_w(lambda c: w_a[:, c, 512:800], lambda c: w_in_v[c, :, 896:1184], 8, 288, R_w)
        w_ukv_v = w_ukv.rearrange("(c p) n -> c p n", p=128)
        for c in range(2):
            s = stg_ring.next()
            P.op("sp", ("dma_start", dict(out=stg[s][:, 0:1024], in_=w_ukv_v[c])), writes=[R_stg[s]], dma="stg%d" % s)
            sv = stg[s][:, 0:1024].rearrange("p (h t d) -> p h t d", h=NH, t=2)
            cast("act", wuk[:, c, :, :], sv[:, :, 0, :], [R_stg[s]], [R_w.new()])
            cast("dve", wuv[:, c, :, :], sv[:, :, 1, :], [R_stg[s]], [R_w.new()])
        P.op("sp", ("dma_start", dict(out=Et[:], in_=sq["E"])), writes=[R_tab], dma=P.uniq("cst"))
        P.op("sp", ("dma_start", dict(out=ropeK[:], in_=sq["rK"])), writes=[R_tab], dma=P.uniq("cst"))

        ubf = [sbt(p1, "ubf_%d" % i, [128, 512], BF16) for i in range(2)]
        R_ubf = [Reg("ubf") for _ in range(2)]
        kvn = [sbt(p1, "kvn_%d" % i, [128, 384], BF16) for i in range(2)]
        R_kvn = [Reg("kvn") for _ in range(2)]
        ckvT4 = [sbt(p1, "ckvT4_%d" % i, [128, 2, 512], BF16) for i in range(2)]
        R_ckvT4 = [Reg("ckvT4") for _ in range(2)]
        krt4 = [sbt(p1, "krt4_%d" % i, [128, 512], BF16) for i in range(2)]
        R_krt4 = [Reg("krt4") for _ in range(2)]
        kst = [sbt(p1, "kst_%d" % i, [128, 4, 512], BF16) for i in range(2)]
        R_kst = [Reg("kst") for _ in range(2)]
        vst = [sbt(p1, "vst_%d" % i, [128, NH, 4, 65], BF16) for i in range(2)]
        R_vst = [Reg("vst") for _ in range(2)]
        for i in range(2):
            P.op("pool", ("memset", dict(ap=kvn[i][:], constant=0.0)), writes=[R_kvn[i]])
            P.op("pool", ("memset", dict(ap=vst[i][:], constant=1.0)), writes=[R_vst[i]])
        R_scr = Reg("scr_" + nm)
        B_T, B_U, B_KV, B_T2, B_K, B_V, B_A0, B_A1 = range(8)
        psA = ps[:, B_A0:B_A0 + 2, :].rearrange("p b n -> p (b n)")

        def load_x(i, src):
            s = i % NX
            P.op("sp", ("dma_start", dict(out=x32[s][:], in_=src[i])), writes=[R_x32[s]], dma="x%d" % s)
            b = i % 2
            P.op("act", ("copy", dict(out=xbf[b][:], in_=x32[s][:])), reads=[R_x32[s]], writes=[R_xbf[b]])

        def make_xT(i):
            b = i % 2
            for c in range(8):
                P.op("pe", ("transpose", dict(out=psbf(B_T)[:, c * 128:(c + 1) * 128], in_=xbf[b][:, c * 128:(c + 1) * 128], identity=ident[:])),
                     reads=[R_xbf[b], R_const], writes=[PB[B_T]])
            P.op("dve", ("tensor_copy", dict(out=xT[b][:].rearrange("p c n -> p (c n)"), in_=psbf(B_T)[:, :])),
                 reads=[PB[B_T]], writes=[R_xT[b]])

        def rms_rstd(b, src_ps, n, breg):
            P.op("act", ("activation", dict(out=junk[:, 0:n], in_=src_ps, func=AF.Square, accum_out=st[b][:, 0:1])),
                 reads=[breg], writes=[R_junk, R_st[b]])
            P.op("act", ("activation", dict(out=st[b][:, 1:2], in_=st[b][:, 0:1], func=AF.Sqrt, scale=1.0 / n, bias=RMS_EPS)),
                 reads=[R_st[b]], writes=[R_st[b]])
            P.op("dve", ("reciprocal", dict(out=st[b][:, 2:3], in_=st[b][:, 1:2])), reads=[R_st[b]], writes=[R_st[b]])

        knv = sq["kn"].rearrange("(hp two) d s -> two d hp s", two=2)
        B_KV1 = 5
        B_VK = 4
        A_regs = [PB[B_A0], PB[B_A1]] if 8 * K2 > 512 else [PB[B_A0]]

        def bkv_of(i):
            return B_KV if i % 2 == 0 else B_KV1

        def s_front(i):
            b = i % 2
            bkv = bkv_of(i)
            make_xT(i)
            for c in range(8):
                P.op("pe", ("matmul", dict(out=ps[:, B_U, :], lhsT=xT[b][:, c, :], rhs=w_a[:, c, 0:512], start=(c == 0), stop=(c == 7))),
                     reads=[R_xT[b], R_w], writes=[PB[B_U]])
            for c in range(8):
                P.op("pe", ("matmul", dict(out=ps[:, bkv, 0:288], lhsT=xT[b][:, c, :], rhs=w_a[:, c, 512:800], start=(c == 0), stop=(c == 7))),
                     reads=[R_xT[b], R_w], writes=[PB[bkv]])

        def s_ubf(i):
            b = i % 2
            P.op("dve", ("tensor_copy", dict(out=ubf[b][:], in_=ps[:, B_U, :])), reads=[PB[B_U]], writes=[R_ubf[b]])

        def s_chain(i):
            b = i % 2
            bkv = bkv_of(i)
            rms_rstd(b, ps[:, bkv, 0:256], 256, PB[bkv])
            P.op("dve", ("scalar_tensor_tensor", dict(out=kvn[b][:, 0:256], in0=ps[:, bkv, 0:256], scalar=st[b][:, 2:3], in1=gkvb[:],
                                                      op0=ALU.mult, op1=ALU.mult)),
                 reads=[PB[bkv], R_st[b], R_const], writes=[R_kvn[b]])
            kr = ps[:, bkv, 256:288]
            P.op("dve", ("tensor_tensor", dict(out=rtmp[b][:, 0:32], in0=kr, in1=ropeK[:, i, 0:32], op=ALU.mult)),
                 reads=[PB[bkv], R_tab], writes=[R_rtmp[b]])
            P.op("dve", ("tensor_tensor", dict(out=rtmp[b][:, 32:48], in0=ps[:, bkv, 272:288], in1=ropeK[:, i, 32:48], op=ALU.mult)),
                 reads=[PB[bkv], R_tab], writes=[R_rtmp[b]])
            P.op("dve", ("tensor_tensor", dict(out=rtmp[b][:, 48:64], in0=ps[:, bkv, 256:272], in1=ropeK[:, i, 48:64], op=ALU.mult)),
                 reads=[PB[bkv], R_tab], writes=[R_rtmp[b]])
            P.op("dve", ("tensor_tensor", dict(out=kvn[b][:, 320:352], in0=rtmp[b][:, 0:32], in1=rtmp[b][:, 32:64], op=ALU.add)),
                 reads=[R_rtmp[b]], writes=[R_kvn[b]])

        def s_back(i):
            b = i % 2
            G, t4 = i // 4, i % 4
            gb = G % 2
            for g in range(4):
                P.op("pe", ("matmul", dict(out=psA[:, g * 2 * K2:(g + 1) * 2 * K2], lhsT=ubf[b][:, g * 128:(g + 1) * 128], rhs=Et[:, i, :],
                                           start=True, stop=True)),
                     reads=[R_ubf[b], R_tab], writes=A_regs)
            P.op("dve", ("tensor_copy", dict(out=Vp[:, :, i], in_=psA[:, 0:8 * K2])), reads=A_regs, writes=[R_Vp])
            for c in range(2):
                P.op("pe", ("transpose", dict(out=psbf(B_T2)[:, c * 128:(c + 1) * 128], in_=kvn[b][:, c * 128:(c + 1) * 128], identity=ident[:])),
                     reads=[R_kvn[b], R_const], writes=[PB[B_T2]])
            P.op("pe", ("transpose", dict(out=psbf(B_T2)[:, 256:384], in_=kvn[b][:, 256:384], identity=ident[:])),
                 reads=[R_kvn[b], R_const], writes=[PB[B_T2]])
            P.op("act", ("copy", dict(out=ckvT4[gb][:, :, t4 * 128:(t4 + 1) * 128], in_=psbf(B_T2)[:, 0:256].rearrange("p (c n) -> p c n", c=2))),
                 reads=[PB[B_T2]], writes=[R_ckvT4[gb]])
            P.op("act", ("copy", dict(out=krt4[gb][64:96, t4 * 128:(t4 + 1) * 128], in_=psbf(B_T2)[64:96, 256:384])),
                 reads=[PB[B_T2]], writes=[R_krt4[gb]])
            for c in range(2):
                P.op("pe", ("matmul", dict(out=ps[:, B_VK, :], lhsT=ckvT4[gb][:, c, t4 * 128:(t4 + 1) * 128],
                                           rhs=wuv[:, c, :, :].rearrange("p h d -> p (h d)"), start=(c == 0), stop=(c == 1))),
                     reads=[R_ckvT4[gb], R_w], writes=[PB[B_VK]])
            P.op("act", ("copy", dict(out=vst[gb][:, :, t4, 0:64], in_=ps[:, B_VK, :].rearrange("p (h d) -> p h d", h=NH))),
                 reads=[PB[B_VK]], writes=[R_vst[gb]])
            if t4 == 3:
                cols = slice(G * 512, (G + 1) * 512)
                for hp in range(4):
                    kbk = B_VK if (hp % 2 == 0 or 8 * K2 > 512) else B_A1
                    for c in range(2):
                        P.op("pe", ("matmul", dict(out=ps[:, kbk, :], lhsT=wuk[:, c, 2 * hp:2 * hp + 2, :].rearrange("p h d -> p (h d)"),
                                                   rhs=ckvT4[gb][:, c, :], start=(c == 0), stop=(c == 1))),
                             reads=[R_ckvT4[gb], R_w], writes=[PB[kbk]])
                    if hp % 2 == 0:
                        P.op("dve", ("tensor_copy", dict(out=kst[gb][:, hp, :], in_=ps[:, kbk, :])),
                             reads=[PB[kbk]], writes=[R_kst[gb]])
                    else:
                        P.op("act", ("copy", dict(out=kst[gb][:, hp, :], in_=ps[:, kbk, :])),
                             reads=[PB[kbk]], writes=[R_kst[gb]])
                for two in range(2):
                    P.op("pool", ("dma_start", dict(out=knv[two][:, :, cols], in_=kst[gb][two * 64:(two + 1) * 64, :, :])),
                         reads=[R_kst[gb]], writes=[R_scr], dma="kst%d" % gb)
                P.op("pool", ("dma_start", dict(out=sq["kr"][:, cols], in_=krt4[gb][64:96, :])),
                     reads=[R_krt4[gb]], writes=[R_scr], dma="kst%d" % gb)
                P.op("pool", ("dma_start", dict(out=sq["vs"][:, :, G, :].rearrange("h p n -> p h n"),
                                                in_=vst[gb][:].rearrange("p h t d -> p h (t d)"))),
                     reads=[R_vst[gb]], writes=[R_scr], dma="vst%d" % gb)

        load_x(0, sq["xa"])
        if R > 1:
            load_x(1, sq["xa"])
        s_front(0)
        s_ubf(0)
        for i in range(R):
            if i + 2 < R:
                load_x(i + 2, sq["xa"])
            if i + 1 < R:
                s_front(i + 1)
            s_chain(i)
            s_back(i)
            if i + 1 < R:
                s_ubf(i + 1)
        P.barrier()
        g_1a.close()
        if stop():
            stopped = True
            break

        p1b = Group(arena)
        QT = sbt(g_qt, "QT_" + nm, [128, NH, T * 128], BF16)
        w_q = sbt(p1b, "w_q", [128, 8, 384], BF16)
        wuq = sbt(p1b, "wuq", [128, 3, 768], BF16)
        ropeQ = [sbt(p1b, "ropeQ%d" % i, [128, 4, NH, 16], F32) for i in range(2)]
        R_ropeQ = [Reg("ropeQ") for _ in range(2)]
        load_w(lambda c: w_q[:, c, :], lambda c: w_in_v[c, :, 512:896], 8, 384, R_w)
        w_uq_v = w_uq.rearrange("(c p) n -> c p n", p=128)
        load_w(lambda c: wuq[:, c, :], lambda c: w_uq_v[c], 3, 768, R_w)
        cqn = [sbt(p1b, "cqn_%d" % i, [128, 384], BF16) for i in range(2)]
        R_cqn = [Reg("cqn") for _ in range(2)]
        cqT = [sbt(p1b, "cqT_%d" % i, [128, 3, 128], BF16) for i in range(2)]
        R_cqT = [Reg("cqT") for _ in range(2)]
        qbf = [sbt(p1b, "qbf_%d" % i, [128, NH, 96], BF16) for i in range(2)]
        R_qbf = [Reg("qbf") for _ in range(2)]
        qf = [sbt(p1b, "qf_%d" % i, [128, 768], F32) for i in range(2)]
        R_qf = [Reg("qf") for _ in range(2)]
        qtmp = [sbt(p1b, "qtmp_%d" % i, [128, NH, 64], F32) for i in range(2)]
        R_qtmp = [Reg("qtmp") for _ in range(2)]
        psQ = ps[:, B_KV:B_KV + 2, :].rearrange("p b n -> p (b n)")
        B_Q0, B_Q1 = B_KV, B_KV + 1
        CQB = [B_U, 4]

        def q_front(j):
            b = j % 2
            cb = CQB[j % 2]
            make_xT(j)
            for c in range(8):
                P.op("pe", ("matmul", dict(out=ps[:, cb, 0:384], lhsT=xT[b][:, c, :], rhs=w_q[:, c, :], start=(c == 0), stop=(c == 7))),
                     reads=[R_xT[b], R_w], writes=[PB[cb]])
            P.op("sp", ("dma_start", dict(out=ropeQ[b][:].rearrange("p a h d -> p (a h d)"), in_=sq["rQ"][:, j, :])),
                 writes=[R_ropeQ[b]], dma="rq%d" % b)

        def q_chain(j):
            b = j % 2
            cb = CQB[j % 2]
            rms_rstd(b, ps[:, cb, 0:384], 384, PB[cb])
            P.op("dve", ("scalar_tensor_tensor", dict(out=cqn[b][:], in0=ps[:, cb, 0:384], scalar=st[b][:, 2:3], in1=gqb[:],
                                                      op0=ALU.mult, op1=ALU.mult)),
                 reads=[PB[cb], R_st[b], R_const], writes=[R_cqn[b]])

        def q_back(j):
            b = j % 2
            for c in range(3):
                P.op("pe", ("transpose", dict(out=psbf(B_A0)[:, c * 128:(c + 1) * 128], in_=cqn[b][:, c * 128:(c + 1) * 128], identity=ident[:])),
                     reads=[R_cqn[b], R_const], writes=[PB[B_A0]])
            P.op("act", ("copy", dict(out=cqT[b][:].rearrange("p c n -> p (c n)"), in_=psbf(B_A0)[:, 0:384])), reads=[PB[B_A0]], writes=[R_cqT[b]])
            for (lo, hi, bank) in [(0, 512, B_Q0), (512, 768, B_Q1)]:
                for c in range(3):
                    P.op("pe", ("matmul", dict(out=psQ[:, lo:hi], lhsT=cqT[b][:, c, :], rhs=wuq[:, c, lo:hi], start=(c == 0), stop=(c == 2))),
                         reads=[R_cqT[b], R_w], writes=[PB[bank]])
            P.op("act", ("copy", dict(out=qf[b][:], in_=psQ[:, 0:768])), reads=[PB[B_Q0], PB[B_Q1]], writes=[R_qf[b]])
            q3 = qf[b][:].rearrange("p (h d) -> p h d", h=NH)
            rq = [R_qf[b]]
            P.op("dve", ("tensor_copy", dict(out=qbf[b][:, :, 0:64], in_=q3[:, :, 0:64])), reads=rq, writes=[R_qbf[b]])
            c1 = ropeQ[b][:, 0, :, :]
            c2 = ropeQ[b][:, 1, :, :]
            s1 = ropeQ[b][:, 2, :, :]
            s2 = ropeQ[b][:, 3, :, :]
            P.op("dve", ("tensor_tensor", dict(out=qtmp[b][:, :, 0:16], in0=q3[:, :, 64:80], in1=c1, op=ALU.mult)), reads=rq + [R_ropeQ[b]], writes=[R_qtmp[b]])
            P.op("dve", ("tensor_tensor", dict(out=qtmp[b][:, :, 16:32], in0=q3[:, :, 80:96], in1=c2, op=ALU.mult)), reads=rq + [R_ropeQ[b]], writes=[R_qtmp[b]])
            P.op("dve", ("tensor_tensor", dict(out=qtmp[b][:, :, 32:48], in0=q3[:, :, 80:96], in1=s1, op=ALU.mult)), reads=rq + [R_ropeQ[b]], writes=[R_qtmp[b]])
            P.op("dve", ("tensor_tensor", dict(out=qtmp[b][:, :, 48:64], in0=q3[:, :, 64:80], in1=s2, op=ALU.mult)), reads=rq + [R_ropeQ[b]], writes=[R_qtmp[b]])
            P.op("dve", ("tensor_tensor", dict(out=qbf[b][:, :, 64:96], in0=qtmp[b][:, :, 0:32], in1=qtmp[b][:, :, 32:64], op=ALU.add)),
                 reads=[R_qtmp[b]], writes=[R_qbf[b]])
            for h in range(NH):
                P.op("pe", ("transpose", dict(out=psbf(B_V)[0:96, h * 128:(h + 1) * 128], in_=qbf[b][:, h, :], identity=ident[:])),
                     reads=[R_qbf[b], R_const], writes=[PB[B_V]])
            P.op("act", ("copy", dict(out=QT[0:96, :, j * 128:(j + 1) * 128], in_=psbf(B_V)[0:96, :].rearrange("p (h n) -> p h n", h=NH))),
                 reads=[PB[B_V]], writes=[R_QT])

        load_x(0, sq["xo"])
        if T > 1:
            load_x(1, sq["xo"])
        q_front(0)
        for j in range(T):
            if j + 2 < T:
                load_x(j + 2, sq["xo"])
            if j + 1 < T:
                q_front(j + 1)
            q_chain(j)
            q_back(j)
        P.barrier()
        p1b.close()
        g_px.close()
        if stop():
            stopped = True
            break

        p2 = Group(arena)
        FT = sbt(g_ft, "FT_" + nm, [128, 4, T * 128], BF16)
        crt = sbt(p2, "crt", [128, 2, 128], BF16)
        P.op("sp", ("dma_start", dict(out=crt[:], in_=sq["cr"])), writes=[R_tab], dma=P.uniq("cst"))
        Zb = [sbt(p2, "Zb_%d" % i, [128, 256], BF16) for i in range(2)]
        R_Zb = [Reg("Zb") for _ in range(2)]
        Vp5 = Vp[:].rearrange("p (g ri k) i -> p g ri k i", g=4, ri=2)
        FTv = FT[:].rearrange("q g (i k1 pk) -> q g i k1 pk", k1=R, pk=kb)
        zi = 0
        for bI in range(K2 // kb):
            yb = 2 + (bI % 2)
            for g in range(4):
                zb = zi % 2
                zi += 1
                for ri in range(2):
                    P.op("pe", ("matmul", dict(out=ps[:, zb, 0:256], lhsT=Vp5[:, g, ri, bI * kb:(bI + 1) * kb, :].rearrange("p k i -> p (k i)"),
                                                             rhs=cht[:, ri, :], start=(ri == 0), stop=(ri == 1))),
                         reads=[R_Vp, R_const], writes=[PB[zb]])
                if zb == 0:
                    P.op("act", ("copy", dict(out=Zb[zb][:], in_=ps[:, zb, 0:256])), reads=[PB[zb]], writes=[R_Zb[zb]])
                else:
                    P.op("dve", ("tensor_copy", dict(out=Zb[zb][:], in_=ps[:, zb, 0:256])), reads=[PB[zb]], writes=[R_Zb[zb]])
                for ri in range(2):
                    P.op("pe", ("matmul", dict(out=ps[:, yb, g * 128:(g + 1) * 128], lhsT=Zb[zb][:, ri * 128:(ri + 1) * 128], rhs=crt[:, ri, :],
                                               start=(ri == 0), stop=(ri == 1))),
                         reads=[R_Zb[zb], R_tab], writes=[PB[yb]])
            i0, p0 = (kb * bI) % T, (kb * bI) // T
            P.op("act", ("copy", dict(out=FTv[:, :, i0:i0 + kb, :, p0],
                                                                    in_=ps[:, yb, 0:4 * kb * R].rearrange("p (g k r) -> p g k r", g=4, k=kb))),
                 reads=[PB[yb]], writes=[R_FT])
        P.barrier()
        p2.close()
        g_vp.close()
        if stop():
            stopped = True
            break

        p3 = Group(arena)
        attnT = sbt(g_at, "attnT_" + nm, [128, 4, T * 128], BF16)
        R_attnT = Reg("attnT")
        NS = 3
        kTt = [sbt(p3, "kTt_%d" % i, [128, 512], BF16) for i in range(NS)]
        Vt = [sbt(p3, "Vt_%d" % i, [128, 4, 128], BF16) for i in range(NS)]
        R_kv = [Reg("kvslot") for _ in range(NS)]
        for i in range(NS):
            P.op("pool", ("memset", dict(ap=Vt[i][:], constant=1.0)), writes=[R_kv[i]])
        pT = [sbt(p3, "pT_%d" % i, [128, 1024], BF16) for i in range(3)]
        R_pT = [Reg("pT") for _ in range(3)]
        rs = sbt(p3, "rs", [128, 512], F32)
        R_rs = Reg("rs")
        osb = [sbt(p3, "osb_%d" % i, [128, 512], F32) for i in range(2)]
        R_osb = [Reg("osb") for _ in range(2)]
        Nown = T * 128
        QW = min(512, Nown)
        nqc = Nown // QW
        qgroups = [list(range(a, min(a + 4, nqc))) for a in range(0, nqc, 4)]
        nkc = S // 512
        ci_box = [0]
        ni = 0
        for h in range(NH):
            for qg in qgroups:
                units = []
                a = 0
                while a < len(qg):
                    if a + 1 < len(qg):
                        units.append([qg[a], qg[a + 1]])
                        a += 2
                    else:
                        units.append([qg[a]])
                        a += 1
                steps = []
                for kc in range(nkc):
                    for kt in range(4):
                        for u in units:
                            steps.append((kc, kt, u))
                slot_of = {}

                def emit_qk(step, sb_):
                    nonlocal_ci = None
                    kc, kt, u = step
                    if kc not in slot_of:
                        s = ci_box[0] % NS
                        ci_box[0] += 1
                        slot_of[kc] = s
                        cols = slice(kc * 512, (kc + 1) * 512)
                        P.op("sp", ("dma_start", dict(out=kTt[s][0:64, :], in_=sq["kn"][h][:, cols])),
                             reads=[R_scr], writes=[R_kv[s]], dma="kv%d" % s)
                        P.op("sp", ("dma_start", dict(out=kTt[s][64:96, :], in_=sq["kr"][:, cols])),
                             reads=[R_scr], writes=[R_kv[s]], dma="kv%d" % s)
                        P.op("sp", ("dma_start", dict(out=Vt[s][:, :, 0:65], in_=sq["vs"][h, :, kc, :].rearrange("p (t d) -> p t d", t=4))),
                             reads=[R_scr], writes=[R_kv[s]], dma="kv%d" % s)
                    s = slot_of[kc]
                    banks = [4 + 2 * sb_, 5 + 2 * sb_]
                    for ui, qc in enumerate(u):
                        P.op("pe", ("matmul", dict(out=ps[:, 4 + 2 * sb_ + ui, 0:QW], lhsT=kTt[s][0:96, kt * 128:(kt + 1) * 128],
                            rhs=QT[0:96, h, qc * QW:(qc + 1) * QW], start=True, stop=True)),
                            reads=[R_kv[s], R_QT], writes=[PB[banks[ui]]])

                def emit_exp(step, sb_, pb_):
                    kc, kt, u = step
                    banks = [4 + 2 * sb_, 5 + 2 * sb_]
                    nb = len(u)
                    if nb == 2 and QW == 512:
                        src = ps[:, 4 + 2 * sb_:6 + 2 * sb_, :].rearrange("p b n -> p (b n)")
                        dst = pT[pb_][:, :]
                    else:
                        src = ps[:, 4 + 2 * sb_:4 + 2 * sb_ + nb, 0:QW]
                        dst = pT[pb_][:, :].rearrange("p (b n) -> p b n", b=2)[:, 0:nb, 0:QW]
                    P.op("act", ("activation", dict(out=dst, in_=src, func=AF.Exp, scale=SCALE)),
                         reads=[PB[bk] for bk in banks[:nb]], writes=[R_pT[pb_]])

                def emit_pv(step, pb_):
                    kc, kt, u = step
                    s = slot_of[kc]
                    for ui, qc in enumerate(u):
                        acc = qg.index(qc)
                        P.op("pe", ("matmul", dict(out=ps[:, acc, 0:QW], lhsT=Vt[s][:, kt, :], rhs=pT[pb_][:, ui * 512:ui * 512 + QW],
                            start=(kc == 0 and kt == 0), stop=(kc == nkc - 1 and kt == 3))),
                            reads=[R_kv[s], R_pT[pb_]], writes=[PB[acc]])

                emit_qk(steps[0], 0)
                if len(steps) > 1:
                    emit_qk(steps[1], 1)
                for ii, stp in enumerate(steps):
                    emit_exp(stp, ii % 2, ii % 3)
                    if ii + 2 < len(steps):
                        emit_qk(steps[ii + 2], ii % 2)
                    emit_pv(stp, ii % 3)
                for qc in qg:
                    acc = qg.index(qc)
                    nb_ = ni % 2
                    ni += 1
                    bcb = 4 + 2 * nb_
                    po = (h % 2) * 64
                    P.op("dve", ("reciprocal", dict(out=osb[nb_][64:128, 0:QW], in_=ps[64:128, acc, 0:QW])), reads=[PB[acc]], writes=[R_osb[nb_]])
                    P.op("dve", ("tensor_tensor", dict(
                        out=attnT[po:po + 64, h // 2, qc * QW:(qc + 1) * QW], in0=ps[0:64, acc, 0:QW], in1=osb[nb_][64:128, 0:QW], op=ALU.mult)),
                        reads=[R_osb[nb_], PB[acc]], writes=[R_attnT])
        P.barrier()
        p3.close()
        g_qt.close()
        if stop():
            stopped = True
            break

        p4 = Group(arena)
        lnb = sbt(p4, "lnb", [128, 2, D], F32)
        P.op("sp", ("dma_start", dict(out=lnb[:], in_=ln_b[:, 0:2, :])), writes=[R_const], dma=P.uniq("cst"))
        wg = sbt(p4, "wg", [128, 8, 2048], BF16)
        wf = sbt(p4, "wf", [128, 4, D], BF16)
        wat = sbt(p4, "wat", [128, 4, D], BF16)
        wo = sbt(p4, "wo", [128, 8, D], BF16)
        R_w4 = MultiReg("w4")
        load_w(lambda c: wg[:, c, :], lambda c: w_in_v[c, :, 1184:3232], 8, 2048, R_w4)
        wfv = w_fourier.rearrange("(c p) n -> c p n", p=128)
        load_w(lambda c: wf[:, c, :], lambda c: wfv[c], 4, D, R_w4)
        wav = w_attn.rearrange("(c p) n -> c p n", p=128)
        load_w(lambda c: wat[:, c, :], lambda c: wav[c], 4, D, R_w4)
        wov = w_o.rearrange("(c p) n -> c p n", p=128)
        load_w(lambda c: wo[:, c, :], lambda c: wov[c], 8, D, R_w4)
        BT = min(2, T)
        W = BT * 128
        nblk = T // BT
        x4 = [sbt(p4, "x4_%d" % i, [128, BT, D], F32) for i in range(2)]
        R_x4 = [Reg("x4") for _ in range(2)]
        xb4 = sbt(p4, "xb4", [128, BT, D], BF16)
        R_xb4 = Reg("xb4")
        xT4 = sbt(p4, "xT4", [128, BT, 8, 128], BF16)
        R_xT4 = Reg("xT4")
        mT = sbt(p4, "mT", [128, 8, W], BF16)
        R_mT = Reg("mT")
        sg = [sbt(p4, "sg_%d" % i, [128, 2, W], F32) for i in range(2)]
        R_sg = [Reg("sg") for _ in range(2)]
        mt = [sbt(p4, "mt_%d" % i, [128, 2, W], F32) for i in range(2)]
        R_mt = [Reg("mt") for _ in range(2)]
        rr = [sbt(p4, "rr_%d" % i, [128, D], F32) for i in range(3)]
        R_rr = [Reg("rr") for _ in range(3)]
        lst = [sbt(p4, "lst_%d" % i, [128, 24], F32) for i in range(3)]
        R_lst = [Reg("lst") for _ in range(3)]
        R_hs = Reg("h_s")

        def layer_norm(b, src, R_src, dst, R_dst, gi):
            L = lst[b]
            ops = []
            for k in range(2):
                ops.append(lambda k=k: P.op("dve", ("bn_stats", dict(out=L[:, k * 6:(k + 1) * 6], in_=src[:, k * 512:(k + 1) * 512])), reads=[R_src], writes=[R_lst[b]]))
            ops.append(lambda: P.op("dve", ("bn_aggr", dict(out=L[:, 12:14], in_=L[:, 0:12].rearrange("p (k s) -> p k s", k=2))), reads=[R_lst[b]], writes=[R_lst[b]]))
            ops.append(lambda: P.op("act", ("activation", dict(out=L[:, 14:15], in_=L[:, 13:14], func=AF.Sqrt, scale=1.0, bias=LN_EPS)), reads=[R_lst[b]], writes=[R_lst[b]]))
            ops.append(lambda: P.op("dve", ("reciprocal", dict(out=L[:, 15:16], in_=L[:, 14:15])), reads=[R_lst[b]], writes=[R_lst[b]]))
            ops.append(lambda: P.op("dve", ("tensor_scalar", dict(out=L[:, 16:17], in0=L[:, 12:13], scalar1=L[:, 15:16], scalar2=-1.0, op0=ALU.mult, op1=ALU.mult)),
                                    reads=[R_lst[b]], writes=[R_lst[b]]))
            ops.append(lambda: P.op("act", ("activation", dict(out=dst, in_=src, func=AF.Identity, scale=L[:, 15:16], bias=L[:, 16:17])),
                                    reads=[R_src, R_lst[b]], writes=[R_dst]))
            ops.append(lambda: P.op("dve", ("tensor_tensor", dict(out=dst, in0=dst, in1=lnb[:, gi, :], op=ALU.mult)), reads=[R_dst, R_const], writes=[R_dst]))
            ops.append(lambda: P.op("dve", ("tensor_tensor", dict(out=dst, in0=dst, in1=lnb[:, gi + 1, :], op=ALU.add)), reads=[R_dst, R_const], writes=[R_dst]))
            return ops

        pend4 = []

        def load_x4(bi):
            xb = bi % 2
            P.op("sp", ("dma_start", dict(out=x4[xb][:], in_=sq["xo"][bi * BT:(bi + 1) * BT].rearrange("t p d -> p t d"))),
                 writes=[R_x4[xb]], dma="x4_%d" % xb)

        ti_box = [0]

        def p4_cast(bi):
            xb = bi % 2
            for t in range(BT):
                P.op("dve", ("tensor_copy", dict(out=xb4[:, t, :], in_=x4[xb][:, t, :])), reads=[R_x4[xb]], writes=[R_xb4])

        def p4_tr(bi):
            for t in range(BT):
                for c in range(8):
                    P.op("pe", ("transpose", dict(out=psbf(t % 2)[:, c * 128:(c + 1) * 128], in_=xb4[:, t, c * 128:(c + 1) * 128], identity=ident[:])),
                         reads=[R_xb4, R_const], writes=[PB[t % 2]])
                P.op("dve", ("tensor_copy", dict(out=xT4[:, t, :, :].rearrange("p c n -> p (c n)"), in_=psbf(t % 2)[:, :])),
                     reads=[PB[t % 2]], writes=[R_xT4])

        def p4_fc(bi):
            cols = slice(bi * W, (bi + 1) * W)
            for fc in range(8):
                pb = 4 * (fc % 2)
                fs = slice(fc * 128, (fc + 1) * 128)
                sb2 = fc % 2
                for gi in range(2):
                    for c in range(8):
                        P.op("pe", ("matmul", dict(out=ps[:, pb + gi, 0:W], lhsT=wg[:, c, gi * 1024 + fc * 128:gi * 1024 + (fc + 1) * 128],
                                                   rhs=xT4[:, :, c, :], start=(c == 0), stop=(c == 7))),
                             reads=[R_w4, R_xT4], writes=[PB[pb + gi]])
                for c in range(4):
                    P.op("pe", ("matmul", dict(out=ps[:, pb + 2, 0:W], lhsT=wf[:, c, fs], rhs=FT[:, c, cols], start=(c == 0), stop=(c == 3))),
                         reads=[R_w4, R_FT], writes=[PB[pb + 2]])
                for c in range(4):
                    P.op("pe", ("matmul", dict(out=ps[:, pb + 3, 0:W], lhsT=wat[:, c, fs], rhs=attnT[:, c, cols], start=(c == 0), stop=(c == 3))),
                         reads=[R_w4, R_attnT], writes=[PB[pb + 3]])
                for gi in range(2):
                    P.op("act", ("activation", dict(out=sg[sb2][:, gi, :], in_=ps[:, pb + gi, 0:W], func=AF.Sigmoid)),
                         reads=[PB[pb + gi]], writes=[R_sg[sb2]])
                for gi in range(2):
                    P.op("dve", ("tensor_tensor", dict(out=mt[sb2][:, gi, :], in0=sg[sb2][:, gi, :], in1=ps[:, pb + 2 + gi, 0:W], op=ALU.mult)),
                         reads=[R_sg[sb2], PB[pb + 2 + gi]], writes=[R_mt[sb2]])
                P.op("pool", ("tensor_tensor", dict(out=mT[:, fc, :], in0=mt[sb2][:, 0, :], in1=mt[sb2][:, 1, :], op=ALU.add)),
                     reads=[R_mt[sb2]], writes=[R_mT])
                for _ in range(3):
                    if fc >= 1 and pend4:
                        pend4.pop(0)()

        def p4_wo(bi):
            xb = bi % 2
            tbs = []
            while pend4:
                pend4.pop(0)()
            for t in range(BT):
                ti = ti_box[0]
                ti_box[0] += 1
                tb = ti % 3
                b0 = 4 * (ti % 2)
                tbs.append(tb)
                for half in range(2):
                    for c in range(8):
                        P.op("pe", ("matmul", dict(out=ps[:, b0 + half, :], lhsT=mT[:, c, t * 128:(t + 1) * 128],
                                                   rhs=wo[:, c, half * 512:(half + 1) * 512], start=(c == 0), stop=(c == 7))),
                             reads=[R_mT, R_w4], writes=[PB[b0 + half]])
                for half in range(2):
                    hsl = slice(half * 512, (half + 1) * 512)
                    P.op("dve", ("scalar_tensor_tensor", dict(out=rr[tb][:, hsl], in0=x4[xb][:, t, hsl], scalar=ALPHA, in1=ps[:, b0 + half, :],
                                                              op0=ALU.mult, op1=ALU.add)),
                         reads=[R_x4[xb], PB[b0 + half]], writes=[R_rr[tb]])
            return tbs

        def p4_ln(bi, tbs):
            for t, tb in enumerate(tbs):
                pend4.extend(layer_norm(tb, rr[tb][:], R_rr[tb], rr[tb][:], R_rr[tb], 0))
                tg = sq["hoff"] + bi * BT + t
                pend4.append(lambda tg=tg, tb=tb: P.op("pool", ("dma_start", dict(out=h_s[tg], in_=rr[tb][:])), reads=[R_rr[tb]], writes=[R_hs], dma="hst%d" % tb))

        load_x4(0)
        if nblk > 1:
            load_x4(1)
        p4_cast(0)
        p4_tr(0)
        for bi in range(nblk):
            p4_fc(bi)
            if bi + 1 < nblk:
                p4_cast(bi + 1)
            tbs = p4_wo(bi)
            if bi + 2 < nblk:
                load_x4(bi + 2)
            if bi + 1 < nblk:
                p4_tr(bi + 1)
            p4_ln(bi, tbs)
        while pend4:
            pend4.pop(0)()
        P.barrier()
        p4.close()
        g_ft.close()
        g_at.close()
        if stop():
            stopped = True
            break

    if stopped:
        arena.free = [(0, 206 * 1024)]
        P.emit()
        return nc
    pf = Group(arena)
    lnb = sbt(pf, "lnb2", [128, 2, D], F32)
    P.op("sp", ("dma_start", dict(out=lnb[:], in_=ln_b[:, 2:4, :])), writes=[R_const], dma=P.uniq("cst"))
    wgt = sbt(pf, "wgt", [128, 8, DFF], BF16)
    wup = sbt(pf, "wup", [128, 8, DFF], BF16)
    wdn = sbt(pf, "wdn", [128, NF, D], BF16)
    R_wf = MultiReg("wffn")
    wgv = w_gate.rearrange("(c p) n -> c p n", p=128)
    wuv_ = w_up.rearrange("(c p) n -> c p n", p=128)
    wdv = w_down.rearrange("(c p) n -> c p n", p=128)
    load_w(lambda c: wgt[:, c, :], lambda c: wgv[c], 8, DFF, R_wf)
    load_w(lambda c: wup[:, c, :], lambda c: wuv_[c], 8, DFF, R_wf)
    load_w(lambda c: wdn[:, c, :], lambda c: wdv[c], NF, D, R_wf)
    FB = 2
    FW = FB * 128
    h4 = [sbt(pf, "h4_%d" % i, [128, FB, D], F32) for i in range(2)]
    R_h4 = [Reg("h4") for _ in range(2)]
    hb4 = sbt(pf, "hb4", [128, FB, D], BF16)
    R_hb4 = Reg("hb4")
    hT = sbt(pf, "hT", [128, FB, 8, 128], BF16)
    R_hT = Reg("hT")
    actT = sbt(pf, "actT", [128, NF, FW], BF16)
    R_actT = Reg("actT")
    sgl = [sbt(pf, "sgl_%d" % i, [128, FW], F32) for i in range(2)]
    R_sgl = [Reg("sgl") for _ in range(2)]
    r2 = [sbt(pf, "r2_%d" % i, [128, D], F32) for i in range(2)]
    R_r2 = [Reg("r2") for _ in range(2)]
    lst2 = [sbt(pf, "lst2_%d" % i, [128, 24], F32) for i in range(2)]
    R_lst2 = [Reg("lst2") for _ in range(2)]
    R_yout = Reg("yout")

    def layer_norm2(b, src, R_src, dst, R_dst, gi):
        L = lst2[b]
        ops = []
        for k in range(2):
            ops.append(lambda k=k: P.op("dve", ("bn_stats", dict(out=L[:, k * 6:(k + 1) * 6], in_=src[:, k * 512:(k + 1) * 512])), reads=[R_src], writes=[R_lst2[b]]))
        ops.append(lambda: P.op("dve", ("bn_aggr", dict(out=L[:, 12:14], in_=L[:, 0:12].rearrange("p (k s) -> p k s", k=2))), reads=[R_lst2[b]], writes=[R_lst2[b]]))
        ops.append(lambda: P.op("act", ("activation", dict(out=L[:, 14:15], in_=L[:, 13:14], func=AF.Sqrt, scale=1.0, bias=LN_EPS)), reads=[R_lst2[b]], writes=[R_lst2[b]]))
        ops.append(lambda: P.op("dve", ("reciprocal", dict(out=L[:, 15:16], in_=L[:, 14:15])), reads=[R_lst2[b]], writes=[R_lst2[b]]))
        ops.append(lambda: P.op("dve", ("tensor_scalar", dict(out=L[:, 16:17], in0=L[:, 12:13], scalar1=L[:, 15:16], scalar2=-1.0, op0=ALU.mult, op1=ALU.mult)),
                                reads=[R_lst2[b]], writes=[R_lst2[b]]))
        ops.append(lambda: P.op("act", ("activation", dict(out=dst, in_=src, func=AF.Identity, scale=L[:, 15:16], bias=L[:, 16:17])),
                                reads=[R_src, R_lst2[b]], writes=[R_dst]))
        ops.append(lambda: P.op("dve", ("tensor_tensor", dict(out=dst, in0=dst, in1=lnb[:, gi, :], op=ALU.mult)), reads=[R_dst, R_const], writes=[R_dst]))
        ops.append(lambda: P.op("dve", ("tensor_tensor", dict(out=dst, in0=dst, in1=lnb[:, gi + 1, :], op=ALU.add)), reads=[R_dst, R_const], writes=[R_dst]))
        return ops

    nfb = NT // FB

    def load_h(bi):
        hb = bi % 2
        P.op("sp", ("dma_start", dict(out=h4[hb][:], in_=h_s[bi * FB:(bi + 1) * FB].rearrange("t p d -> p t d"))),
             reads=[R_hs], writes=[R_h4[hb]], dma="h4_%d" % hb)

    cnt = {"gi": 0, "di": 0}

    def f_cast(bi):
        hb = bi % 2
        for t in range(FB):
            P.op("dve", ("tensor_copy", dict(out=hb4[:, t, :], in_=h4[hb][:, t, :])), reads=[R_h4[hb]], writes=[R_hb4])

    def f_tr(bi):
        for t in range(FB):
            for c in range(8):
                P.op("pe", ("transpose", dict(out=psbf(t % 2)[:, c * 128:(c + 1) * 128], in_=hb4[:, t, c * 128:(c + 1) * 128], identity=ident[:])),
                     reads=[R_hb4, R_const], writes=[PB[t % 2]])
            P.op("dve", ("tensor_copy", dict(out=hT[:, t, :, :].rearrange("p c n -> p (c n)"), in_=psbf(t % 2)[:, :])),
                 reads=[PB[t % 2]], writes=[R_hT])

    def f_gu(bi):
        for f in range(NF):
            gb_ = 2 + (cnt["gi"] % 2)
            sb3 = cnt["gi"] % 2
            cnt["gi"] += 1
            for (wm, off) in [(wgt, 0), (wup, FW)]:
                for c in range(8):
                    P.op("pe", ("matmul", dict(out=ps[:, gb_, off:off + FW], lhsT=wm[:, c, f * 128:(f + 1) * 128], rhs=hT[:, :, c, :],
                                               start=(c == 0), stop=(c == 7))),
                         reads=[R_wf, R_hT], writes=[PB[gb_]])
            P.op("act", ("activation", dict(out=sgl[sb3][:], in_=ps[:, gb_, 0:FW], func=AF.Silu)), reads=[PB[gb_]], writes=[R_sgl[sb3]])
            P.op("dve", ("tensor_tensor", dict(out=actT[:, f, :], in0=sgl[sb3][:], in1=ps[:, gb_, FW:2 * FW], op=ALU.mult)),
                 reads=[R_sgl[sb3], PB[gb_]], writes=[R_actT])
            if f >= 2 and pend:
                pend.pop(0)()

    def f_down(bi):
        hb = bi % 2
        dbs = []
        while pend:
            pend.pop(0)()
        for t in range(FB):
            db = cnt["di"] % 2
            cnt["di"] += 1
            dbs.append(db)
            b0 = 4 + 2 * db
            for half in range(2):
                for f in range(NF):
                    P.op("pe", ("matmul", dict(out=ps[:, b0 + half, :], lhsT=actT[:, f, t * 128:(t + 1) * 128],
                                               rhs=wdn[:, f, half * 512:(half + 1) * 512], start=(f == 0), stop=(f == NF - 1))),
                         reads=[R_actT, R_wf], writes=[PB[b0 + half]])
            for half in range(2):
                hsl = slice(half * 512, (half + 1) * 512)
                P.op("dve", ("scalar_tensor_tensor", dict(out=r2[db][:, hsl], in0=h4[hb][:, t, hsl], scalar=ALPHA, in1=ps[:, b0 + half, :],
                                                          op0=ALU.mult, op1=ALU.add)),
                     reads=[R_h4[hb], PB[b0 + half]], writes=[R_r2[db]])
        return dbs

    pend = []

    def f_ln(bi, dbs):
        for t, db in enumerate(dbs):
            pend.extend(layer_norm2(db, r2[db][:], R_r2[db], r2[db][:], R_r2[db], 0))
            tg = bi * FB + t
            dst = yp[tg] if tg < Tp else ys[tg - Tp]
            pend.append(lambda dst=dst, db=db: P.op("pool", ("dma_start", dict(out=dst, in_=r2[db][:])), reads=[R_r2[db]], writes=[R_yout], dma="yst%d" % db))

    load_h(0)
    if nfb > 1:
        load_h(1)
    f_cast(0)
    f_tr(0)
    for bi in range(nfb):
        f_gu(bi)
        if bi + 1 < nfb:
            f_cast(bi + 1)
        dbs = f_down(bi)
        if bi + 2 < nfb:
            load_h(bi + 2)
        if bi + 1 < nfb:
            f_tr(bi + 1)
        f_ln(bi, dbs)
    while pend:
        pend.pop(0)()
    P.emit()
    print("arena peak", arena.peak, flush=True)
    pf.close()
    top.close()
    return nc


def _bf(a):
    return np.asarray(a, dtype=np.float32).astype(ml_dtypes.bfloat16)


def _tile_order(x, R):
    S, d = x.shape
    return np.ascontiguousarray(x.reshape(128, R, d).transpose(1, 0, 2))


def _E_table(R, k2s):
    S = 128 * R
    i = np.arange(R)[None, :, None]
    p = np.arange(128)[:, None, None]
    n = i + R * p
    k2 = np.asarray(k2s)[None, None, :]
    m = (n * k2) % S
    th = 2 * np.pi * m / S
    s = 1.0 / np.sqrt(128.0)
    E = np.concatenate([np.cos(th) * s, -np.sin(th) * s], axis=2)
    return _bf(E)


def _ch_table():
    c = np.arange(128)
    th = 2 * np.pi * ((c[:, None] * c[None, :]) % 128) / 128
    s = 1.0 / np.sqrt(128.0)
    C, Sn = np.cos(th) * s, np.sin(th) * s
    T1 = np.concatenate([C, -Sn], axis=1)
    T2 = np.concatenate([Sn, C], axis=1)
    return _bf(np.stack([T1, T2], axis=1))


def _cr_table(R):
    q = np.arange(128)
    blk_r, n1 = q // R, q % R
    blk_c, k1 = q // R, q % R
    th = 2 * np.pi * ((n1[:, None] * k1[None, :]) % R) / R
    s = 1.0 / np.sqrt(float(R))
    mask = (blk_r[:, None] == blk_c[None, :]).astype(np.float64)
    return _bf(np.stack([np.cos(th) * s * mask, np.sin(th) * s * mask], axis=1))


def _rope_table(R, tiles):
    inv = (1.0 / (np.float32(10000.0) ** (np.arange(0, 32, 2, dtype=np.float32) / np.float32(32)))).astype(np.float32)
    pos = (np.asarray(tiles)[None, :] + R * np.arange(128)[:, None]).astype(np.float32)
    ang = (pos[:, :, None] * inv[None, None, :]).astype(np.float32)
    c, s = np.cos(ang).astype(np.float32), np.sin(ang).astype(np.float32)
    return np.ascontiguousarray(np.concatenate([c, c, -s, s], axis=2).astype(np.float32))


def _rope_rep(t):
    p, nt, _ = t.shape
    a = t.reshape(p, nt, 4, 1, 16)
    return np.ascontiguousarray(np.broadcast_to(a, (p, nt, 4, NH, 16)).reshape(p, nt, 4 * NH * 16)).astype(np.float32)


_NC_CACHE = {}


def run(inputs, Rp, Rs, dbg=False, limit=99):
    key = (Rp, Rs, limit)
    if key not in _NC_CACHE:
        _NC_CACHE[key] = build(Rp, Rs, limit=limit)
    nc = _NC_CACHE[key]
    Tp = Rp // NCORES
    f32 = lambda a: np.ascontiguousarray(np.asarray(a, dtype=np.float32))
    xp = _tile_order(f32(inputs["x_prompt"])[0], Rp)
    xs = f32(inputs["x_sample"])
    shared = dict(
        xp_all=xp,
        w_in=f32(inputs["w_in"])[0], w_fourier=f32(inputs["w_fourier"])[0], w_uq=f32(inputs["w_uq"])[0],
        w_ukv=f32(inputs["w_ukv"])[0], w_attn=f32(inputs["w_attn"])[0], w_o=f32(inputs["w_o"])[0],
        w_gate=f32(inputs["w_gate"])[0], w_up=f32(inputs["w_up"])[0], w_down=f32(inputs["w_down"])[0],
        gq_b=np.ascontiguousarray(np.broadcast_to(f32(inputs["g_q"])[0][None, :], (128, 384))),
        gkv_b=np.ascontiguousarray(np.broadcast_to(f32(inputs["g_kv"])[0][None, :], (128, 256))),
        ln_b=np.ascontiguousarray(np.broadcast_to(
            np.stack([f32(inputs[k])[0] for k in ["ln1_g", "ln1_b", "ln2_g", "ln2_b"]])[None], (128, 4, D))),
        ident=_bf(np.eye(128)),
        Es=_E_table(Rs, np.arange(128)),
        cht=_ch_table(), crp=_cr_table(Rp), crs=_cr_table(Rs),
        ropeKp=_rope_table(Rp, np.arange(Rp)), ropeKs=_rope_table(Rs, np.arange(Rs)),
    )
    ropeQs = _rope_rep(shared["ropeKs"])
    in_maps = []
    for r in range(NCORES):
        own = np.arange(r * Tp, (r + 1) * Tp)
        m = dict(shared)
        m["xp_own"] = np.ascontiguousarray(xp[own])
        m["xs_all"] = _tile_order(xs[r], Rs)
        kbp = 128 // Rp
        m["Ep"] = _E_table(Rp, np.concatenate([own + Rp * pk for pk in range(kbp)]))
        m["ropeQp"] = _rope_rep(shared["ropeKp"][:, own, :])
        m["ropeQs"] = ropeQs
        in_maps.append(m)
    res = run_bass_kernel_spmd(nc, in_maps, core_ids=list(range(NCORES)))
    yp = np.zeros((Rp, 128, D), np.float32)
    ys = np.zeros((NCORES, 128 * Rs, D), np.float32)
    for r in range(NCORES):
        o = res.results[r]
        yp[r * Tp:(r + 1) * Tp] = o["yp"]
        ys[r] = np.asarray(o["ys"]).transpose(1, 0, 2).reshape(128 * Rs, D)
    y_prompt = yp.transpose(1, 0, 2).reshape(1, 128 * Rp, D)
    return (np.ascontiguousarray(y_prompt), ys)


def kernel(**inputs):
    Rp = inputs["x_prompt"].shape[1] // 128
    Rs = inputs["x_sample"].shape[1] // 128
    return run(inputs, Rp, Rs)
```
